# Optimizing a Trainium2 kernel written in Bass

```python
import jax
import jax.numpy as jnp
from jax import lax
import numpy as np

D_MODEL = 1024
BATCH = 8
SEQ = 4096
DEPTH = 1

PLE_DIM = 256
D_FF = 2816
RET_HEADS = 4
RET_QK_DIM = 64
RET_V_DIM = 128
RET_CHUNK = 128
SWA_Q_HEADS = 8
SWA_KV_HEADS = 2
SWA_HEAD_DIM = 64
SWA_WINDOW = 128
ROPE_BASE = 10000.0
EPS = 1e-6
NEG_INF = -1e30

RET_QK_W = RET_HEADS * RET_QK_DIM
RET_V_W = RET_HEADS * RET_V_DIM
SWA_Q_W = SWA_Q_HEADS * SWA_HEAD_DIM
SWA_KV_W = SWA_KV_HEADS * SWA_HEAD_DIM
MIX_W = RET_V_W + SWA_Q_W
IN_W = 2 * RET_QK_W + 2 * RET_V_W + SWA_Q_W + 2 * SWA_KV_W
SPLIT_POINTS = (
    RET_QK_W,
    2 * RET_QK_W,
    2 * RET_QK_W + RET_V_W,
    2 * RET_QK_W + 2 * RET_V_W,
    2 * RET_QK_W + 2 * RET_V_W + SWA_Q_W,
    2 * RET_QK_W + 2 * RET_V_W + SWA_Q_W + SWA_KV_W,
)

kernel_name = 'hymba_style_retention_swa_sink_macaron'


def _rmsnorm(x, g):
    x32 = x.astype(jnp.float32)
    y = x32 * lax.rsqrt(jnp.mean(x32 * x32, axis=-1, keepdims=True) + EPS)
    return y.astype(x.dtype) * g


def _swiglu(x, w_gate, w_up, w_down):
    return (jax.nn.silu(x @ w_gate) * (x @ w_up)) @ w_down


def _rotary(x, pos):
    half = x.shape[-1] // 2
    inv_freq = ROPE_BASE ** (-jnp.arange(half, dtype=jnp.float32) / half)
    ang = pos.astype(jnp.float32)[:, :, None, None] * inv_freq
    cos, sin = jnp.cos(ang), jnp.sin(ang)
    x32 = x.astype(jnp.float32)
    x1, x2 = x32[..., :half], x32[..., half:]
    return jnp.concatenate([x1 * cos - x2 * sin, x2 * cos + x1 * sin], axis=-1).astype(x.dtype)


def _head_groupnorm(y):
    y32 = y.astype(jnp.float32)
    mu = jnp.mean(y32, axis=-1, keepdims=True)
    var = jnp.mean(jnp.square(y32 - mu), axis=-1, keepdims=True)
    return ((y32 - mu) * lax.rsqrt(var + EPS)).astype(y.dtype)


def _retention_chunkwise(q, k, v):
    B, S, H, dk = q.shape
    dv = v.shape[-1]
    C = RET_CHUNK
    N = S // C
    log_gamma = jnp.log(1.0 - 2.0 ** (-5.0 - jnp.arange(H, dtype=jnp.float32)))
    idx = jnp.arange(C, dtype=jnp.float32)
    diff = idx[:, None] - idx[None, :]
    causal = diff >= 0
    decay_intra = jnp.where(causal[None], jnp.exp(log_gamma[:, None, None] * jnp.where(causal, diff, 0.0)[None]), 0.0)
    q_decay = jnp.exp(log_gamma[:, None] * (idx[None, :] + 1.0))
    k_decay = jnp.exp(log_gamma[:, None] * (C - 1.0 - idx[None, :]))
    chunk_decay = jnp.exp(log_gamma * C)

    qc = q.astype(jnp.float32).reshape(B, N, C, H, dk)
    kc = k.astype(jnp.float32).reshape(B, N, C, H, dk) * (dk ** -0.5)
    vc = v.astype(jnp.float32).reshape(B, N, C, H, dv)

    scores = jnp.einsum('bnchd,bnmhd->bnhcm', qc, kc) * decay_intra
    intra = jnp.einsum('bnhcm,bnmhe->bnche', scores, vc)
    kv = jnp.einsum('bnmhd,hm,bnmhe->bnhde', kc, k_decay, vc)

    def step(state, kv_n):
        return chunk_decay[:, None, None] * state + kv_n, state

    init = jnp.zeros((B, H, dk, dv), jnp.float32)
    _, state_prev = lax.scan(step, init, jnp.moveaxis(kv, 1, 0))
    state_prev = jnp.moveaxis(state_prev, 0, 1)
    cross = jnp.einsum('bnchd,hc,bnhde->bnche', qc, q_decay, state_prev)
    return (intra + cross).reshape(B, S, H, dv).astype(v.dtype)


def _swa_with_sinks(q, k, v, sinks):
    B, S, Hq, D = q.shape
    Hkv = k.shape[2]
    G = Hq // Hkv
    W = SWA_WINDOW
    N = S // W
    qb = q.reshape(B, N, W, Hkv, G, D)
    pad = ((0, 0), (1, 0), (0, 0), (0, 0), (0, 0))
    kp = jnp.pad(k.reshape(B, N, W, Hkv, D), pad)
    vp = jnp.pad(v.reshape(B, N, W, Hkv, D), pad)
    kwin = jnp.concatenate([kp[:, :-1], kp[:, 1:]], axis=2)
    vwin = jnp.concatenate([vp[:, :-1], vp[:, 1:]], axis=2)

    scores = jnp.einsum('bnqhgd,bnkhd->bnhgqk', qb, kwin).astype(jnp.float32) * (D ** -0.5)
    qi = jnp.arange(W)[:, None] + W
    kj = jnp.arange(2 * W)[None, :]
    rel = qi - kj
    valid = (rel >= 0) & (rel < W)
    blk_valid = valid[None] & ((jnp.arange(N)[:, None, None] > 0) | (kj >= W)[None])
    scores = jnp.where(blk_valid[None, :, None, None], scores, NEG_INF)

    sink = sinks.astype(jnp.float32).reshape(Hkv, G)[None, None, :, :, None, None]
    m = jnp.maximum(jnp.max(scores, axis=-1, keepdims=True), sink)
    e = jnp.exp(scores - m)
    denom = jnp.sum(e, axis=-1, keepdims=True) + jnp.exp(sink - m)
    probs = (e / denom).astype(v.dtype)
    out = jnp.einsum('bnhgqk,bnkhd->bnqhgd', probs, vwin)
    return out.reshape(B, S, Hq * D)


def setup_inputs(seed: int = 0) -> dict:
    key = jax.random.key(seed)
    ks = jax.random.split(key, 24)
    f32 = jnp.float32

    def w(k, shape, fan_in):
        return jax.random.normal(k, shape, f32) * (fan_in ** -0.5)

    def gain(k):
        return 1.0 + 0.02 * jax.random.normal(k, (DEPTH, D_MODEL), f32)

    x = jax.random.normal(ks[0], (BATCH, SEQ, D_MODEL), f32)
    p = jax.random.normal(ks[1], (DEPTH, BATCH, SEQ, PLE_DIM), f32)
    positions = jnp.tile(jnp.arange(SEQ, dtype=jnp.int32)[None, :], (BATCH, 1))
    return {
        'x': x,
        'p': p,
        'positions': positions,
        'ffn1_pre_g': gain(ks[2]),
        'ffn1_post_g': gain(ks[3]),
        'ffn1_w_gate': w(ks[4], (DEPTH, D_MODEL, D_FF), D_MODEL),
        'ffn1_w_up': w(ks[5], (DEPTH, D_MODEL, D_FF), D_MODEL),
        'ffn1_w_down': w(ks[6], (DEPTH, D_FF, D_MODEL), D_FF),
        'mix_pre_g': gain(ks[7]),
        'mix_post_g': gain(ks[8]),
        'w_in': w(ks[9], (DEPTH, D_MODEL, IN_W), D_MODEL),
        'b_in': 0.02 * jax.random.normal(ks[10], (DEPTH, IN_W), f32),
        'swa_sinks': 0.5 * jax.random.normal(ks[11], (DEPTH, SWA_Q_HEADS), f32),
        'w_out': w(ks[12], (DEPTH, MIX_W, D_MODEL), MIX_W),
        'ffn2_pre_g': gain(ks[13]),
        'ffn2_post_g': gain(ks[14]),
        'ffn2_w_gate': w(ks[15], (DEPTH, D_MODEL, D_FF), D_MODEL),
        'ffn2_w_up': w(ks[16], (DEPTH, D_MODEL, D_FF), D_MODEL),
        'ffn2_w_down': w(ks[17], (DEPTH, D_FF, D_MODEL), D_FF),
        'ple_w_proj': w(ks[18], (DEPTH, PLE_DIM, D_MODEL), PLE_DIM),
        'ple_w_gate': w(ks[19], (DEPTH, D_MODEL, D_MODEL), D_MODEL),
        'ple_norm_g': gain(ks[20]),
    }


def reference(x, p, positions, ffn1_pre_g, ffn1_post_g, ffn1_w_gate, ffn1_w_up, ffn1_w_down,
              mix_pre_g, mix_post_g, w_in, b_in, swa_sinks, w_out,
              ffn2_pre_g, ffn2_post_g, ffn2_w_gate, ffn2_w_up, ffn2_w_down,
              ple_w_proj, ple_w_gate, ple_norm_g):
    B, S, _ = x.shape
    h = x
    for i in range(DEPTH):
        a = _rmsnorm(h, ffn1_pre_g[i])
        h = h + 0.5 * _rmsnorm(_swiglu(a, ffn1_w_gate[i], ffn1_w_up[i], ffn1_w_down[i]), ffn1_post_g[i])

        u = _rmsnorm(h, mix_pre_g[i])
        z = u @ w_in[i] + b_in[i]
        rq, rk, rv, rg, sq, sk, sv = jnp.split(z, SPLIT_POINTS, axis=-1)

        rq = _rotary(rq.reshape(B, S, RET_HEADS, RET_QK_DIM), positions)
        rk = _rotary(rk.reshape(B, S, RET_HEADS, RET_QK_DIM), positions)
        rv = rv.reshape(B, S, RET_HEADS, RET_V_DIM)
        ret = _head_groupnorm(_retention_chunkwise(rq, rk, rv)).reshape(B, S, RET_V_W)
        ret = jax.nn.silu(rg) * ret

        swa = _swa_with_sinks(sq.reshape(B, S, SWA_Q_HEADS, SWA_HEAD_DIM),
                              sk.reshape(B, S, SWA_KV_HEADS, SWA_HEAD_DIM),
                              sv.reshape(B, S, SWA_KV_HEADS, SWA_HEAD_DIM),
                              swa_sinks[i])

        mix = jnp.concatenate([ret, swa], axis=-1) @ w_out[i]
        h = h + _rmsnorm(mix, mix_post_g[i])

        a = _rmsnorm(h, ffn2_pre_g[i])
        h = h + 0.5 * _rmsnorm(_swiglu(a, ffn2_w_gate[i], ffn2_w_up[i], ffn2_w_down[i]), ffn2_post_g[i])

        gate = jax.nn.sigmoid(h @ ple_w_gate[i])
        h = h + _rmsnorm(gate * (p[i] @ ple_w_proj[i]), ple_norm_g[i])
    return h
```

```python
import numpy as np
import ml_dtypes
from contextlib import ExitStack
import concourse.bass as bass
import concourse.mybir as mybir
from concourse.bass_utils import run_bass_kernel_spmd

F32, BF16, I32 = mybir.dt.float32, mybir.dt.bfloat16, mybir.dt.int32
AF = mybir.ActivationFunctionType
ALU = mybir.AluOpType
AX = mybir.AxisListType

D = 1024
S = 4096
DFF = 2816
TB = 512
NT = TB // 128
NBLK = S // TB
RING = 16
UE = 2048
EPS = 1e-6
NU_FFN = 33
NU = 87
PRE_G = 3
NEG = -30000.0
STOP_AFTER = 99
NBLK_RUN = NBLK
MIX_LEVEL = 5
MIX_SUB = 9
PERM = (0, 2, 1, 3)

C_GPRE = 0
C_BIASF = 24
C_INVF = 38
C_NSGN = 39
C_SGNPI = 40
C_GAMC = 41
C_CAUS = 43
C_SWAM = 171
C_SWAM0 = 427
C_QDEC = 683
C_KDEC = 939
C_NHALF = 1195
C_PI = 1196
NC_ = 1197
B_GPOST = 0
B_BIAS = 4096
B_SINK = 5248
NB_ = 5256


class T:
    __slots__ = ("name", "w", "r", "excl")

    def __init__(self, name, excl=False):
        self.name = name
        self.w = None
        self.r = {}
        self.excl = excl


class Sched:
    ENG = ("pe", "act", "dve", "pool", "sp")

    def __init__(self, nc):
        self.nc = nc
        self.ops = {e: [] for e in self.ENG}
        self.cnt = {e: 0 for e in self.ENG}
        self.seen = {e: {} for e in self.ENG}
        self.dcnt = {}
        self.rot = {}

    def _needs(self, reads, writes):
        need = {}

        def req(k, v):
            if need.get(k, 0) < v:
                need[k] = v
        W = list(writes)
        Rd = []
        for t in reads:
            (W if t.excl else Rd).append(t)
        for t in Rd:
            if t.w is not None:
                req(*t.w)
        for t in W:
            if t.w is not None:
                req(*t.w)
            for k, v in t.r.items():
                req(k, v)
        return need, Rd, W

    def op(self, eng, fn, reads=(), writes=()):
        need, Rd, W = self._needs(reads, writes)
        seq = self.cnt[eng] + 1
        waits = []
        for k, v in need.items():
            if k == eng and (eng == "pe" or seq - v > 3):
                continue
            if self.seen[eng].get(k, 0) >= v:
                continue
            self.seen[eng][k] = v
            waits.append((k, v))
        self.cnt[eng] = seq
        self.ops[eng].append((waits, fn, False))
        for t in Rd:
            if t.r.get(eng, 0) < seq:
                t.r[eng] = seq
        for t in W:
            t.w = (eng, seq)
            t.r = {}

    def dma(self, q, out, in_, reads=(), writes=(), sem=None, pool=("d", 4)):
        if sem is None:
            name, n = pool
            i = self.rot.get(name, 0)
            self.rot[name] = i + 1
            sem = f"{name}{i % n}"
        need, Rd, W = self._needs(reads, writes)
        prev = self.dcnt.get(sem, 0)
        if prev > 0 and need.get(sem, 0) < prev:
            need[sem] = prev
        val = prev + 16
        self.dcnt[sem] = val
        waits = []
        for k, v in need.items():
            if self.seen[q].get(k, 0) >= v:
                continue
            self.seen[q][k] = v
            waits.append((k, v))
        self.ops[q].append((waits, (out, in_, sem), True))
        for t in Rd:
            if t.r.get(sem, 0) < val:
                t.r[sem] = val
        for t in W:
            t.w = (sem, val)
            t.r = {}

    def emit(self, stack):
        nc = self.nc
        ENG = self.ENG
        waited = {e: set() for e in ENG}
        for e in ENG:
            for waits, fn, isdma in self.ops[e]:
                for k, v in waits:
                    if k in ENG:
                        waited[k].add(v)
        rank = {e: {v: i + 1 for i, v in enumerate(sorted(waited[e]))} for e in ENG}
        semh = {}
        for e in ENG:
            semh[e] = stack.enter_context(nc.semaphore("E_" + e))
        for k in self.dcnt:
            semh[k] = stack.enter_context(nc.semaphore("D_" + k))
        fin = [(k, v) for k, v in self.dcnt.items()]
        self.ops["sp"].append((fin, None, False))

        def run(ename, eobj):
            seq = 0
            for waits, fn, isdma in self.ops[ename]:
                for k, v in waits:
                    eobj.wait_ge(semh[k], rank[k][v] if k in ENG else v)
                if fn is None:
                    continue
                if isdma:
                    out, in_, sem = fn
                    eobj.dma_start(out=out, in_=in_).then_inc(semh[sem], 16)
                else:
                    ins = fn(eobj)
                    seq += 1
                    if seq in rank[ename]:
                        ins.then_inc(semh[ename], 1)

        with nc.Block() as block:
            block.tensor(lambda e: run("pe", e))
            block.scalar(lambda e: run("act", e))
            block.vector(lambda e: run("dve", e))
            block.gpsimd(lambda e: run("pool", e))
            block.sync(lambda e: run("sp", e))


def build_program():
    nc = bass.Bass("TRN2", target_bir_lowering=False)
    x = nc.dram_tensor("x", [S, D], F32, kind="ExternalInput").ap()
    pin = nc.dram_tensor("p", [S, 256], F32, kind="ExternalInput").ap()
    pos = nc.dram_tensor("pos", [1, S], I32, kind="ExternalInput").ap()
    wu = nc.dram_tensor("wu", [NU * 128, UE], F32, kind="ExternalInput").ap()
    cst_d = nc.dram_tensor("cst", [128, NC_], F32, kind="ExternalInput").ap()
    bc_d = nc.dram_tensor("bc", [128, NB_], F32, kind="ExternalInput").ap()
    ident_d = nc.dram_tensor("ident", [128, 128], BF16, kind="ExternalInput").ap()
    y = nc.dram_tensor("y", [S, D], F32, kind="ExternalOutput").ap()
    scr = nc.dram_tensor("scr", [NU * 128, UE], BF16, kind="Internal").ap()

    sc = Sched(nc)
    with ExitStack() as st:
        def sb(name, shape, dt):
            return st.enter_context(nc.sbuf_tensor("s_" + name, shape, dt))

        h = sb("h", [128, NT, D], F32)
        aT = sb("aT", [128, 2, 8, TB], BF16)
        hT = sb("hT", [128, 22, TB], BF16)
        ring = sb("ring", [128, RING, UE], BF16)
        cst = sb("cst", [128, NC_], F32)
        bcs = sb("bcs", [128, NB_], F32)
        ident = sb("ident", [128, 128], BF16)
        xn = sb("xn", [128, 2, D], BF16)
        tmp = sb("tmp", [128, 2, D], F32)
        sg = sb("sg", [128, 2, TB], F32)
        st_ms = sb("st_ms", [128, 16, 2], F32)
        st_r = sb("st_r", [128, 16, 2], F32)
        qdT = sb("qdT", [128, 2, TB], BF16)
        kiT = sb("kiT", [128, 2, TB], BF16)
        sqT = sb("sqT", [128, 4, TB], BF16)
        skT = sb("skT", [128, 2, 2 * TB], BF16)
        svt = sb("svt", [128, 8, 128], BF16)
        vtok = sb("vtok", [128, 2, 512], BF16)
        rgs = sb("rgs", [128, 2, 512], F32)
        kitok = sb("kitok", [128, 2, 256], BF16)
        sTm = sb("sTm", [128, 2, 512], BF16)
        stU = sb("stU", [128, 2, 128], F32)
        stB = sb("stB", [128, 2, 2, 128], BF16)
        gst = sb("gst", [128, 4, 4], F32)
        swS = sb("swS", [128, 8, 256], F32)
        swP = sb("swP", [128, 8, 256], BF16)
        swPT = sb("swPT", [128, 16, 128], BF16)
        ang = swS[:, 0:2, :].rearrange("p a b -> p (a b)")
        cos2 = swS[:, 2:4, :].rearrange("p a b -> p (a b)")
        sin2 = swS[:, 4:6, :].rearrange("p a b -> p (a b)")
        posi = ang.bitcast(I32)
        swst = sb("swst", [128, 2, 8, 8], F32)
        ptok = sb("ptok", [128, 2, 256], F32)
        pbf = sb("pbf", [128, 2, 256], BF16)
        pT = sb("pT", [128, 2, TB], BF16)
        psum = [st.enter_context(nc.psum_tensor(f"ps{i}", [128, 1024], F32)) for i in range(4)]

        rA, rB = tmp[:, 0, 0:512], tmp[:, 1, 0:512]
        osb, osq = tmp[:, 0, 512:1024], tmp[:, 1, 512:1024]
        mixt = xn
        t_h = [T(f"h{i}") for i in range(NT)]
        t_aT = [[T(f"aT{a}_{i}") for i in range(NT)] for a in range(2)]
        t_hT = [T(f"hT{j}") for j in range(22)]
        t_ring = [T(f"ring{s}") for s in range(RING)]
        t_scr = [T(f"scr{g}") for g in range((NU + PRE_G - 1) // PRE_G)]
        t_cst = T("cst")
        t_xn = [T("xn0"), T("xn1")]
        t_tmp = [T("tmp0"), T("tmp1")]
        t_sg = [T("sg0"), T("sg1")]
        t_ms = [T(f"ms{i}") for i in range(16)]
        t_r = [T(f"r{i}") for i in range(16)]
        t_bank = [T(f"bank{i}", excl=True) for i in range(8)]
        t_rot = None
        t_qd = [T("qd0"), T("qd1")]
        t_ki = [T("ki0"), T("ki1")]
        t_sq = [T(f"sq{c}") for c in range(4)]
        t_sk = [[T(f"sk{kv}_{g}") for g in range(8)] for kv in range(2)]
        t_sv = [T(f"sv{g}") for g in range(8)]
        t_v = [T("v0"), T("v1")]
        t_rg = [T("rg0"), T("rg1")]
        t_kit = [T("kit0"), T("kit1")]
        t_sTm = [T("sTm0"), T("sTm1")]
        t_U = [T("U0"), T("U1")]
        t_sB = [[T(f"sB{a}_{hp}") for hp in range(2)] for a in range(2)]
        t_osb = t_tmp[0]
        t_osq = t_tmp[1]
        t_gst = T("gst")
        t_swS = T("swS")
        t_rot = t_swS
        t_swP = T("swP")
        t_swPT = T("swPT")
        t_swst = [T("swst0"), T("swst1")]
        t_mix = t_xn
        t_ptok = [T("ptok0"), T("ptok1")]
        t_pbf = [T("pbf0"), T("pbf1")]
        t_pT = [T(f"pT{i}") for i in range(NT)]

        def bank(i):
            return psum[i // 2][:, (i % 2) * 512:(i % 2) * 512 + 512]

        def bankb(i):
            return bank(i).bitcast(BF16)

        cnt = {"ms": 0, "r": 0}

        def new_ms():
            i = cnt["ms"] % 16
            cnt["ms"] += 1
            return st_ms[:, i, :], t_ms[i]

        def new_r():
            i = cnt["r"] % 16
            cnt["r"] += 1
            return st_r[:, i, :], t_r[i]

        sc.dma("sp", cst[:], cst_d, writes=[t_cst], sem="c0")
        sc.dma("sp", bcs[:], bc_d, writes=[t_cst], sem="c1")
        sc.dma("sp", ident[:], ident_d, writes=[t_cst], sem="c2")
        for hp in range(2):
            sc.op("pool", lambda e, hp=hp: e.memset(stU[:, hp, :], 0.0), writes=[t_U[hp]])
            sc.op("pool", lambda e, hp=hp: e.memset(stB[:, 0, hp, :], 0.0), writes=[t_sB[0][hp]])
            sc.op("pool", lambda e, hp=hp: e.memset(stB[:, 1, hp, :], 0.0), writes=[t_sB[1][hp]])
        for kv in range(2):
            sc.op("pool", lambda e, kv=kv: e.memset(skT[:, kv, :], 0.0), writes=t_sk[kv])
        sc.op("pool", lambda e: e.memset(svt[:].rearrange("p a b -> p (a b)"), 0.0), writes=t_sv)
        pre = {"next": 0}
        NGRP = len(t_scr)

        def prepass_upto(q):
            q = min(q, NGRP - 1)
            while pre["next"] <= q:
                g = pre["next"]
                pre["next"] += 1
                r0 = g * PRE_G * 128
                r1 = min(NU, (g + 1) * PRE_G) * 128
                sc.dma("pool", scr[r0:r1, :], wu[r0:r1, :], reads=(t_h if g == 0 else ()), writes=[t_scr[g]], pool=("pp", 10))

        wst = {"next_load": 0, "next_use": 0}
        total_units = NU * NBLK_RUN

        def load_unit(g):
            if g >= total_units:
                return
            u = g % NU
            s = g % RING
            if g < NU:
                prepass_upto(u // PRE_G + 3)
            sc.dma("sp", ring[:, s, :], scr[u * 128:(u + 1) * 128, :], reads=[t_scr[u // PRE_G]], writes=[t_ring[s]], sem=f"w{s}")


        def take_unit():
            g = wst["next_use"]
            wst["next_use"] += 1
            return g

        def release_units(gs):
            for g in gs:
                load_unit(g + RING)

        def uslot(g):
            return g % RING

        def skip_units(n):
            gs = [take_unit() for _ in range(n)]
            release_units(gs)

        def prenorm_T(i, site, abuf, do_norm=True):
            xi = i % 2
            if do_norm:
                ms, tms = new_ms()
                r, tr = new_r()
                sc.op("act", lambda e: e.activation(out=xn[:, xi, :], in_=h[:, i, :], func=AF.Square, scale=1.0 / 32.0, accum_out=ms[:, 0:1]),
                      reads=[t_h[i]], writes=[t_xn[xi], tms])
                sc.op("pool", lambda e: e.tensor_scalar(out=r[:, 0:1], in0=ms[:, 0:1], scalar1=EPS, scalar2=None, op0=ALU.add), reads=[tms], writes=[tr])
                sc.op("pool", lambda e: e.tensor_tensor(out=r[:, 1:2], in0=r[:, 0:1], in1=cst[:, C_NHALF:C_NHALF + 1], op=ALU.pow), reads=[tr, t_cst], writes=[tr])
                sc.op("dve", lambda e: e.tensor_scalar(out=xn[:, xi, :], in0=h[:, i, :], scalar1=r[:, 1:2], scalar2=None, op0=ALU.mult),
                      reads=[t_h[i], tr], writes=[t_xn[xi]])
            else:
                sc.op("dve", lambda e: e.tensor_copy(out=xn[:, xi, :], in_=h[:, i, :]), reads=[t_h[i]], writes=[t_xn[xi]])
            b = i % 4
            for k in range(8):
                sc.op("pe", lambda e, k=k: e.transpose(bankb(b)[:, k * 128:(k + 1) * 128], xn[:, xi, k * 128:(k + 1) * 128], ident[:]),
                      reads=[t_xn[xi], t_cst], writes=[t_bank[b]])
            dst = aT[:, abuf, :, i * 128:(i + 1) * 128]
            src = bankb(b).rearrange("p (k c) -> p k c", k=8)
            if do_norm:
                gp = cst[:, C_GPRE + site * 8:C_GPRE + site * 8 + 8].unsqueeze(2).broadcast_to([128, 8, 128])
                sc.op("dve", lambda e: e.tensor_tensor(out=dst, in0=src, in1=gp, op=ALU.mult), reads=[t_bank[b], t_cst], writes=[t_aT[abuf][i]])
            else:
                sc.op("act", lambda e: e.copy(out=dst, in_=src), reads=[t_bank[b]], writes=[t_aT[abuf][i]])

        def postnorm_add(i, bk, gidx, src_sb=None, half=False):
            ti = i % 2
            sq_scale = (2.0 if half else 1.0) / 32.0
            eps_v = (4.0 if half else 1.0) * EPS
            ms, tms = new_ms()
            r, tr = new_r()
            if src_sb is None:
                f = psum[bk // 2][:, :]
                rd = [t_bank[bk], t_bank[bk + 1]]
                for hh in range(2):
                    sc.op("act", lambda e, hh=hh: e.activation(out=xn[:, ti, hh * 512:(hh + 1) * 512], in_=bank(bk + hh), func=AF.Square, scale=sq_scale, accum_out=ms[:, hh:hh + 1]),
                          reads=[t_bank[bk + hh]], writes=[t_xn[ti], tms])
                sc.op("pool", lambda e: e.tensor_tensor(out=r[:, 0:1], in0=ms[:, 0:1], in1=ms[:, 1:2], op=ALU.add), reads=[tms], writes=[tr])
                sc.op("pool", lambda e: e.tensor_scalar(out=r[:, 0:1], in0=r[:, 0:1], scalar1=eps_v, scalar2=None, op0=ALU.add), reads=[tr], writes=[tr])
            else:
                f, tsrc = src_sb
                rd = [tsrc]
                sc.op("act", lambda e: e.activation(out=xn[:, ti, :], in_=f, func=AF.Square, scale=sq_scale, accum_out=ms[:, 0:1]),
                      reads=[tsrc], writes=[t_xn[ti], tms])
                sc.op("pool", lambda e: e.tensor_scalar(out=r[:, 0:1], in0=ms[:, 0:1], scalar1=eps_v, scalar2=None, op0=ALU.add), reads=[tms], writes=[tr])
            sc.op("pool", lambda e: e.tensor_tensor(out=r[:, 1:2], in0=r[:, 0:1], in1=cst[:, C_NHALF:C_NHALF + 1], op=ALU.pow), reads=[tr, t_cst], writes=[tr])
            gb = bcs[:, B_GPOST + gidx * 1024:B_GPOST + (gidx + 1) * 1024]
            sc.op("dve", lambda e: e.scalar_tensor_tensor(out=tmp[:, ti, :], in0=f, scalar=r[:, 1:2], in1=gb, op0=ALU.mult, op1=ALU.mult),
                  reads=rd + [tr, t_cst], writes=[t_tmp[ti]])
            sc.op("dve", lambda e: e.tensor_tensor(out=h[:, i, :], in0=h[:, i, :], in1=tmp[:, ti, :], op=ALU.add), reads=[t_tmp[ti]], writes=[t_h[i]])

        def ffn(site_pre, gpost, abuf, do_pre=True, next_pre=None):
            if do_pre:
                for i in range(NT):
                    prenorm_T(i, site_pre, abuf)
            rd_a = t_aT[abuf]
            gus = []
            for j in range(22):
                g = take_unit()
                s = uslot(g)
                bg, bu = (0, 1) if j % 2 == 0 else (2, 3)
                for half, bb in ((0, bg), (1, bu)):
                    for k in range(8):
                        sc.op("pe", lambda e, k=k, half=half, bb=bb, s=s: e.matmul(bank(bb), lhsT=ring[:, s, half * 1024 + k * 128: half * 1024 + (k + 1) * 128],
                                                                                   rhs=aT[:, abuf, k, :], start=(k == 0), stop=(k == 7)),
                              reads=[t_ring[s]] + rd_a, writes=[t_bank[bb]])
                release_units([g])
                sj = j % 2
                sc.op("act", lambda e, bg=bg, sj=sj: e.activation(out=sg[:, sj, :], in_=bank(bg), func=AF.Silu), reads=[t_bank[bg]], writes=[t_sg[sj]])
                sc.op("dve", lambda e, bu=bu, sj=sj, j=j: e.tensor_tensor(out=hT[:, j, :], in0=bank(bu), in1=sg[:, sj, :], op=ALU.mult),
                      reads=[t_bank[bu], t_sg[sj]], writes=[t_hT[j]])
            dus = [take_unit() for _ in range(11)]
            for i in range(NT):
                bk = 4 + 2 * (i % 2)
                for half in range(2):
                    for k in range(22):
                        s = uslot(dus[k // 2])
                        sc.op("pe", lambda e, k=k, half=half, s=s, i=i, bk=bk: e.matmul(bank(bk + half), lhsT=hT[:, k, i * 128:(i + 1) * 128],
                                                                                         rhs=ring[:, s, (k % 2) * 1024 + half * 512:(k % 2) * 1024 + half * 512 + 512],
                                                                                         start=(k == 0), stop=(k == 21)),
                              reads=[t_ring[s], t_hT[k]], writes=[t_bank[bk + half]])
                if i == NT - 1:
                    release_units(dus)
                if next_pre is not None and i >= 2:
                    next_pre(i - 2)
                postnorm_add(i, bk, gpost, half=True)
            if next_pre is not None:
                next_pre(NT - 2)
                next_pre(NT - 1)

        def tok_major_mm(i, units, kk, n, abuf, bk, nk=8):
            for half in range((n + 511) // 512):
                w = min(512, n - half * 512)
                for k in range(nk):
                    s = uslot(units[k // kk])
                    off = (k % kk) * n + half * 512
                    sc.op("pe", lambda e, k=k, s=s, off=off, w=w, half=half: e.matmul(bank(bk + half)[:, 0:w], lhsT=aT[:, abuf, k, i * 128:(i + 1) * 128],
                                                                                       rhs=ring[:, s, off:off + w], start=(k == 0), stop=(k == nk - 1)),
                          reads=[t_ring[s], t_aT[abuf][i]], writes=[t_bank[bk + half]])

        def mixer(blk, abuf_in, abuf_out, do_pre=True, next_pre=None):
            t0 = blk * TB
            if do_pre:
                for i in range(NT):
                    prenorm_T(i, 1, abuf_in)
            rd_a = t_aT[abuf_in]
            sc.dma("sp", posi[:], pos[:, t0:t0 + TB].broadcast_to([128, TB]), writes=[t_rot], pool=("m", 2))
            sc.op("dve", lambda e: e.tensor_copy(out=ang[:], in_=posi[:]), reads=[t_rot], writes=[t_rot])
            sc.op("dve", lambda e: e.tensor_scalar(out=ang[:], in0=ang[:], scalar1=cst[:, C_INVF:C_INVF + 1], scalar2=None, op0=ALU.mult), reads=[t_rot, t_cst], writes=[t_rot])
            TWO_PI = 2.0 * np.pi
            C1 = 6.28125
            C2 = TWO_PI - C1
            kf = rB
            ki = rA.bitcast(I32)

            def range_reduce(dst, shift):
                sc.op("dve", lambda e: e.tensor_scalar(out=dst[:], in0=ang[:], scalar1=shift, scalar2=1.0 / TWO_PI, op0=ALU.add, op1=ALU.mult), reads=[t_rot], writes=[t_rot])
                sc.op("dve", lambda e: e.tensor_copy(out=ki, in_=dst[:]), reads=[t_rot], writes=[t_tmp[0]])
                sc.op("dve", lambda e: e.tensor_copy(out=kf, in_=ki), reads=[t_tmp[0]], writes=[t_tmp[1]])
                sc.op("dve", lambda e: e.scalar_tensor_tensor(out=dst[:], in0=kf, scalar=-C1, in1=ang[:], op0=ALU.mult, op1=ALU.add), reads=[t_tmp[1], t_rot], writes=[t_rot])
                sc.op("dve", lambda e: e.scalar_tensor_tensor(out=dst[:], in0=kf, scalar=-C2, in1=dst[:], op0=ALU.mult, op1=ALU.add), reads=[t_tmp[1], t_rot], writes=[t_rot])
                if shift != 0.0:
                    sc.op("dve", lambda e: e.tensor_scalar(out=dst[:], in0=dst[:], scalar1=shift, scalar2=None, op0=ALU.add), reads=[t_rot], writes=[t_rot])
                sc.op("dve", lambda e: e.tensor_scalar(out=kf, in0=dst[:], scalar1=float(np.pi), scalar2=-TWO_PI, op0=ALU.is_gt, op1=ALU.mult), reads=[t_rot], writes=[t_tmp[1]])
                sc.op("dve", lambda e: e.tensor_tensor(out=dst[:], in0=dst[:], in1=kf, op=ALU.add), reads=[t_rot, t_tmp[1]], writes=[t_rot])
                sc.op("dve", lambda e: e.tensor_scalar(out=kf, in0=dst[:], scalar1=-float(np.pi), scalar2=TWO_PI, op0=ALU.is_lt, op1=ALU.mult), reads=[t_rot], writes=[t_tmp[1]])
                sc.op("dve", lambda e: e.tensor_tensor(out=dst[:], in0=dst[:], in1=kf, op=ALU.add), reads=[t_rot, t_tmp[1]], writes=[t_rot])

            range_reduce(sin2, 0.0)
            sc.op("act", lambda e: e.activation(out=sin2[:], in_=sin2[:], func=AF.Sin, scale=cst[:, C_NSGN:C_NSGN + 1]), reads=[t_rot, t_cst], writes=[t_rot])
            range_reduce(cos2, 0.5 * np.pi)
            sc.op("act", lambda e: e.activation(out=cos2[:], in_=cos2[:], func=AF.Sin), reads=[t_rot], writes=[t_rot])

            fus = [take_unit() for _ in range(7)]

            def fchunk(u, half, bb):
                s = uslot(fus[u])
                for k in range(8):
                    sc.op("pe", lambda e, k=k: e.matmul(bank(bb), lhsT=ring[:, s, half * 1024 + k * 128: half * 1024 + (k + 1) * 128], rhs=aT[:, abuf_in, k, :], start=(k == 0), stop=(k == 7)),
                          reads=[t_ring[s]] + rd_a, writes=[t_bank[bb]])

            for u in range(4):
                hp = u % 2
                isk = u >= 2
                b0, b1 = (0, 1) if u % 2 == 0 else (2, 3)
                fchunk(u, 0, b0)
                fchunk(u, 1, b1)
                cb = C_BIASF + 2 * u
                sc.op("dve", lambda e, b0=b0, cb=cb: e.scalar_tensor_tensor(out=rA[:], in0=bank(b0), scalar=cst[:, cb:cb + 1], in1=cos2[:], op0=ALU.add, op1=ALU.mult),
                      reads=[t_bank[b0], t_cst, t_rot], writes=[t_tmp[0]])
                sc.op("dve", lambda e, b1=b1, cb=cb: e.scalar_tensor_tensor(out=rB[:], in0=bank(b1), scalar=cst[:, cb + 1:cb + 2], in1=sin2[:], op0=ALU.add, op1=ALU.mult),
                      reads=[t_bank[b1], t_cst, t_rot], writes=[t_tmp[1]])
                sc.op("pool", lambda e: e.tensor_tensor(out=rA[:], in0=rA[:], in1=rB[:], op=ALU.add), reads=[t_tmp[1]], writes=[t_tmp[0]])
                dcol = (C_KDEC if isk else C_QDEC) + hp * 128
                dst = (kiT if isk else qdT)[:, hp, :].rearrange("p (n c) -> p n c", n=NT)
                dec = cst[:, dcol:dcol + 128].unsqueeze(1).broadcast_to([128, NT, 128])
                sc.op("pool", lambda e, dst=dst, dec=dec: e.tensor_tensor(out=dst, in0=rA[:].rearrange("p (n c) -> p n c", n=NT), in1=dec, op=ALU.mult),
                      reads=[t_tmp[0], t_cst], writes=[(t_ki if isk else t_qd)[hp]])
            for c in range(4):
                bb = c % 4
                fchunk(4 + c // 2, c % 2, bb)
                cb = C_BIASF + 8 + c
                sc.op("act", lambda e, bb=bb, cb=cb, c=c: e.activation(out=sqT[:, c, :], in_=bank(bb), func=AF.Identity, bias=cst[:, cb:cb + 1], scale=1.0),
                      reads=[t_bank[bb], t_cst], writes=[t_sq[c]])
            par = blk % 2
            for kv in range(2):
                bb = kv
                fchunk(6, kv, bb)
                cb = C_BIASF + 12 + kv
                sc.op("act", lambda e, bb=bb, cb=cb, kv=kv: e.activation(out=skT[:, kv, par * TB:(par + 1) * TB], in_=bank(bb), func=AF.Identity, bias=cst[:, cb:cb + 1], scale=1.0),
                      reads=[t_bank[bb], t_cst], writes=[t_sk[kv][par * 4 + n] for n in range(4)])
            release_units(fus)

            if MIX_LEVEL < 2:
                skip_units(9)
                return
            tus = [take_unit() for _ in range(5)]
            def make_swa(n):
                gn = blk * NT + n
                g8 = gn % 8
                p8 = (gn - 1) % 8
                a2 = n % 2
                cs = slice(n * 128, (n + 1) * 128)
                mcol = C_SWAM0 if gn == 0 else C_SWAM
                SBANK = ((2, 3), (0, 1))
                sw = swst[:, gn % 2]
                tsw = t_swst[gn % 2]

                def swa_front():
                    for rnd in range(2):
                        kv = rnd
                        for hq in range(4):
                            hd = rnd * 4 + hq
                            c, b64 = hd // 2, (hd % 2) * 64
                            bb = SBANK[rnd][hq % 2]
                            col = (hq // 2) * 256
                            sc.op("pe", lambda e, c=c, b64=b64, bb=bb, col=col, kv=kv: e.matmul(bank(bb)[:, col:col + 128], lhsT=sqT[b64:b64 + 64, c, cs], rhs=skT[b64:b64 + 64, kv, p8 * 128:(p8 + 1) * 128], start=True, stop=True),
                                  reads=[t_sq[c], t_sk[kv][p8]], writes=[t_bank[bb]])
                            sc.op("pe", lambda e, c=c, b64=b64, bb=bb, col=col, kv=kv: e.matmul(bank(bb)[:, col + 128:col + 256], lhsT=sqT[b64:b64 + 64, c, cs], rhs=skT[b64:b64 + 64, kv, g8 * 128:(g8 + 1) * 128], start=True, stop=True),
                                  reads=[t_sq[c], t_sk[kv][g8]], writes=[t_bank[bb]])
                    mb = cst[:, mcol:mcol + 256].unsqueeze(1).broadcast_to([128, 2, 256])
                    for rnd in range(2):
                        for hh in range(2):
                            bb = SBANK[rnd][hh]
                            s0 = rnd * 4 + 2 * hh
                            sc.op("dve", lambda e, bb=bb, s0=s0: e.scalar_tensor_tensor(out=swS[:, s0:s0 + 2, :], in0=bank(bb).rearrange("p (h c) -> p h c", h=2), scalar=0.125, in1=mb, op0=ALU.mult, op1=ALU.add),
                                  reads=[t_bank[bb], t_cst], writes=[t_swS])
                    sk_ = bcs[:, B_SINK:B_SINK + 8]
                    sc.op("dve", lambda e: e.tensor_reduce(out=sw[:, :, 0], in_=swS[:], axis=AX.X, op=ALU.max), reads=[t_swS], writes=[tsw])
                    sc.op("dve", lambda e: e.tensor_tensor(out=sw[:, :, 0], in0=sw[:, :, 0], in1=sk_, op=ALU.max), reads=[tsw, t_cst], writes=[tsw])
                    sc.op("dve", lambda e: e.tensor_scalar(out=sw[:, :, 1], in0=sw[:, :, 0], scalar1=-1.0, scalar2=None, op0=ALU.mult), reads=[tsw], writes=[tsw])
                    sc.op("dve", lambda e: e.tensor_tensor(out=sw[:, :, 2], in0=sk_, in1=sw[:, :, 0], op=ALU.subtract), reads=[tsw, t_cst], writes=[tsw])
                def swa_front_b():
                    for s_ in range(8):
                        sc.op("act", lambda e, s_=s_: e.activation(out=swP[:, s_, :], in_=swS[:, s_, :], func=AF.Exp, bias=sw[:, s_, 1:2], scale=1.0, accum_out=sw[:, s_, 3:4]),
                              reads=[t_swS, tsw], writes=[t_swP, tsw])
                    sc.op("act", lambda e: e.activation(out=sw[:, :, 2], in_=sw[:, :, 2], func=AF.Exp), reads=[tsw], writes=[tsw])

                def swa_back():
                    for s_ in range(8):
                        pb = 4 + s_ // 4
                        for kb in range(2):
                            sc.op("pe", lambda e, s_=s_, kb=kb, pb=pb: e.transpose(bankb(pb)[:, ((s_ % 4) * 2 + kb) * 128:((s_ % 4) * 2 + kb + 1) * 128], swP[:, s_, kb * 128:(kb + 1) * 128], ident[:]),
                                  reads=[t_swP, t_cst], writes=[t_bank[pb]])
                    for rnd in range(2):
                        sc.op("act", lambda e, rnd=rnd: e.copy(out=swPT[:, rnd * 8:(rnd + 1) * 8, :].rearrange("p a b -> p (a b)"), in_=bankb(4 + rnd)), reads=[t_bank[4 + rnd]], writes=[t_swPT])

                def swa_back_b():
                    sc.op("dve", lambda e: e.tensor_tensor(out=sw[:, :, 3], in0=sw[:, :, 3], in1=sw[:, :, 2], op=ALU.add), reads=[tsw], writes=[tsw])
                    sc.op("dve", lambda e: e.reciprocal(out=sw[:, :, 4], in_=sw[:, :, 3]), reads=[tsw], writes=[tsw])
                    for s_ in range(8):
                        kv = s_ // 4
                        sc.op("pe", lambda e, s_=s_, kv=kv: e.matmul(bank(7)[:, s_ * 64:(s_ + 1) * 64], lhsT=swPT[:, s_ * 2, :], rhs=svt[:, p8, kv * 64:(kv + 1) * 64], start=True, stop=False),
                              reads=[t_swPT, t_sv[p8]], writes=[t_bank[7]])
                        sc.op("pe", lambda e, s_=s_, kv=kv: e.matmul(bank(7)[:, s_ * 64:(s_ + 1) * 64], lhsT=swPT[:, s_ * 2 + 1, :], rhs=svt[:, g8, kv * 64:(kv + 1) * 64], start=False, stop=True),
                              reads=[t_swPT, t_sv[g8]], writes=[t_bank[7]])
                    sc.op("dve", lambda e: e.tensor_tensor(out=mixt[:, a2, 512:1024].rearrange("p (h d) -> p h d", h=8), in0=bank(7).rearrange("p (h d) -> p h d", h=8),
                                                           in1=sw[:, :, 4].unsqueeze(2).broadcast_to([128, 8, 64]), op=ALU.mult),
                          reads=[t_bank[7], tsw], writes=[t_mix[a2]])

                return swa_front, swa_front_b, swa_back, swa_back_b

            swa_fb = [make_swa(n) for n in range(NT)]

            def tokmajor(n):
                gn = blk * NT + n
                g8 = gn % 8
                a2 = n % 2
                tok_major_mm(n, tus[0:2], 4, 512, abuf_in, 4)
                sc.op("dve", lambda e, a2=a2: e.tensor_tensor(out=vtok[:, a2, :], in0=bank(4), in1=bcs[:, B_BIAS:B_BIAS + 512], op=ALU.add), reads=[t_bank[4], t_cst], writes=[t_v[a2]])
                tok_major_mm(n, tus[4:5], 8, 128, abuf_in, 6)
                sc.op("dve", lambda e, g8=g8: e.tensor_tensor(out=svt[:, g8, :], in0=bank(6)[:, 0:128], in1=bcs[:, B_BIAS + 1024:B_BIAS + 1152], op=ALU.add), reads=[t_bank[6], t_cst], writes=[t_sv[g8]])
                tok_major_mm(n, tus[2:4], 4, 512, abuf_in, 5)
                sc.op("dve", lambda e, a2=a2: e.tensor_tensor(out=rgs[:, a2, :], in0=bank(5), in1=bcs[:, B_BIAS + 512:B_BIAS + 1024], op=ALU.add), reads=[t_bank[5], t_cst], writes=[t_rg[a2]])
                sc.op("act", lambda e, a2=a2: e.activation(out=sg[:, 0, :], in_=rgs[:, a2, :], func=AF.Tanh, scale=0.5), reads=[t_rg[a2]], writes=[t_sg[0]])
                sc.op("dve", lambda e, a2=a2: e.scalar_tensor_tensor(out=rgs[:, a2, :], in0=sg[:, 0, :], scalar=1.0, in1=rgs[:, a2, :], op0=ALU.add, op1=ALU.mult), reads=[t_sg[0], t_rg[a2]], writes=[t_rg[a2]])
                if n == NT - 1:
                    release_units(tus)

            def tile_body(n):
                gn = blk * NT + n
                g8 = gn % 8
                p8 = (gn - 1) % 8
                a2 = n % 2
                cs = slice(n * 128, (n + 1) * 128)
                mcol = C_SWAM0 if gn == 0 else C_SWAM
                SBANK = ((2, 3), (0, 1))

                if MIX_LEVEL >= 4 and n == 0:
                    swa_fb[0][0]()
                    swa_fb[0][1]()
                if n == 0 or MIX_LEVEL < 4:
                    tokmajor(n)
                cs = slice(n * 128, (n + 1) * 128)
                if MIX_LEVEL < 3:
                    return
                for hp in range(2):
                    sc.op("pe", lambda e, hp=hp: e.transpose(bankb(3)[:, 512 + hp * 128:512 + (hp + 1) * 128], kiT[:, hp, cs], ident[:]), reads=[t_ki[hp], t_cst], writes=[t_bank[3]])
                sc.op("act", lambda e, a2=a2: e.copy(out=kitok[:, a2, :], in_=bankb(3)[:, 512:768]), reads=[t_bank[3]], writes=[t_kit[a2]])
                if MIX_SUB < 2:
                    return
                for hd in range(4):
                    hp, b64 = hd // 2, (hd % 2) * 64
                    sbk = 7 if hd % 2 == 0 else 3
                    sc.op("pe", lambda e, hd=hd, hp=hp, b64=b64, sbk=sbk: e.matmul(bank(sbk)[:, (hd // 2) * 128:(hd // 2 + 1) * 128], lhsT=kiT[b64:b64 + 64, hp, cs], rhs=qdT[b64:b64 + 64, hp, cs], start=True, stop=True),
                          reads=[t_ki[hp], t_qd[hp]], writes=[t_bank[sbk]])
                caus = cst[:, C_CAUS:C_CAUS + 128].unsqueeze(1).broadcast_to([128, 2, 128])
                for par, sbk in ((0, 7), (1, 3)):
                    sc.op("dve", lambda e, a2=a2, par=par, sbk=sbk: e.tensor_tensor(out=sTm[:, a2, par * 256:(par + 1) * 256].rearrange("p (h c) -> p h c", h=2), in0=bank(sbk)[:, 0:256].rearrange("p (h c) -> p h c", h=2), in1=caus, op=ALU.mult),
                          reads=[t_bank[sbk], t_cst], writes=[t_sTm[a2]])
                if MIX_SUB < 3:
                    return
                for hp in range(2):
                    sc.op("pe", lambda e, hp=hp, a2=a2: e.matmul(bank(1)[:, hp * 256:(hp + 1) * 256], lhsT=kitok[:, a2, hp * 128:(hp + 1) * 128], rhs=vtok[:, a2, hp * 256:(hp + 1) * 256], start=True, stop=True),
                          reads=[t_kit[a2], t_v[a2]], writes=[t_bank[1]])
                if MIX_SUB < 4:
                    return
                sa = gn % 2
                for hd in range(4):
                    hp, b64 = hd // 2, (hd % 2) * 64
                    sc.op("pe", lambda e, hd=hd, a2=a2: e.matmul(bank(0)[:, hd * 128:(hd + 1) * 128], lhsT=sTm[:, a2, ((hd % 2) * 2 + hd // 2) * 128:((hd % 2) * 2 + hd // 2 + 1) * 128], rhs=vtok[:, a2, hd * 128:(hd + 1) * 128], start=True, stop=False),
                          reads=[t_sTm[a2], t_v[a2]], writes=[t_bank[0]])
                    sc.op("pe", lambda e, hd=hd, hp=hp, b64=b64, sa=sa: e.matmul(bank(0)[:, hd * 128:(hd + 1) * 128], lhsT=qdT[b64:b64 + 64, hp, cs], rhs=stB[b64:b64 + 64, sa, hp, :], start=False, stop=True),
                          reads=[t_qd[hp], t_sB[sa][hp]], writes=[t_bank[0]])
                if MIX_LEVEL >= 5 and n >= 1:
                    mix_T(n - 1)
                if MIX_SUB < 5:
                    return
                for hp in range(2):
                    gc = cst[:, C_GAMC + hp:C_GAMC + hp + 1]
                    for hh in range(2):
                        ps_ = slice(hh * 64, hh * 64 + 64)
                        sc.op("dve", lambda e, hp=hp, hh=hh, ps_=ps_: e.scalar_tensor_tensor(out=stU[ps_, hp, :], in0=stU[ps_, hp, :], scalar=cst[ps_, C_GAMC + hp:C_GAMC + hp + 1],
                                                                                         in1=bank(1)[ps_, hp * 256 + hh * 128:hp * 256 + hh * 128 + 128], op0=ALU.mult, op1=ALU.add),
                              reads=[t_bank[1], t_cst], writes=[t_U[hp]])
                    sc.op("dve", lambda e, hp=hp, gc=gc, sa=sa: e.tensor_scalar(out=stB[:, 1 - sa, hp, :], in0=stU[:, hp, :], scalar1=gc, scalar2=None, op0=ALU.mult),
                          reads=[t_U[hp], t_cst], writes=[t_sB[1 - sa][hp]])
                if MIX_SUB < 6:
                    return
                sc.op("act", lambda e: e.copy(out=osb[:], in_=bank(0)), reads=[t_bank[0]], writes=[t_osb])
                sc.op("act", lambda e: e.activation(out=osq[:], in_=osb[:], func=AF.Square), reads=[t_osb], writes=[t_osq])
                o3 = osb[:].rearrange("p (h c) -> p h c", h=4)
                sc.op("dve", lambda e: e.tensor_reduce(out=gst[:, :, 0], in_=o3, axis=AX.X, op=ALU.add), reads=[t_osb], writes=[t_gst])
                sc.op("dve", lambda e: e.tensor_reduce(out=gst[:, :, 1], in_=osq[:].rearrange("p (h c) -> p h c", h=4), axis=AX.X, op=ALU.add), reads=[t_osq], writes=[t_gst])
                sc.op("pool", lambda e: e.tensor_scalar(out=gst[:, :, 0], in0=gst[:, :, 0], scalar1=1.0 / 128.0, scalar2=None, op0=ALU.mult), reads=[t_gst], writes=[t_gst])
                sc.op("pool", lambda e: e.tensor_tensor(out=gst[:, :, 2], in0=gst[:, :, 0], in1=gst[:, :, 0], op=ALU.mult), reads=[t_gst], writes=[t_gst])
                sc.op("pool", lambda e: e.tensor_scalar(out=gst[:, :, 1], in0=gst[:, :, 1], scalar1=1.0 / 128.0, scalar2=EPS, op0=ALU.mult, op1=ALU.add), reads=[t_gst], writes=[t_gst])
                sc.op("pool", lambda e: e.tensor_tensor(out=gst[:, :, 1], in0=gst[:, :, 1], in1=gst[:, :, 2], op=ALU.subtract), reads=[t_gst], writes=[t_gst])
                sc.op("pool", lambda e: e.tensor_tensor(out=gst[:, :, 3], in0=gst[:, :, 1], in1=cst[:, C_NHALF:C_NHALF + 1].broadcast_to([128, 4]), op=ALU.pow), reads=[t_gst, t_cst], writes=[t_gst])
                sc.op("pool", lambda e: e.tensor_scalar(out=gst[:, :, 3], in0=gst[:, :, 3], scalar1=0.5, scalar2=None, op0=ALU.mult), reads=[t_gst], writes=[t_gst])
                sc.op("dve", lambda e: e.tensor_tensor(out=o3, in0=o3, in1=gst[:, :, 0].unsqueeze(2).broadcast_to([128, 4, 128]), op=ALU.subtract), reads=[t_gst, t_osb], writes=[t_osb])
                sc.op("dve", lambda e: e.tensor_tensor(out=o3, in0=o3, in1=gst[:, :, 3].unsqueeze(2).broadcast_to([128, 4, 128]), op=ALU.mult), reads=[t_gst, t_osb], writes=[t_osb])
                sc.op("pool", lambda e, a2=a2: e.tensor_tensor(out=mixt[:, a2, 0:512], in0=osb[:], in1=rgs[:, a2, :], op=ALU.mult), reads=[t_osb, t_rg[a2]], writes=[t_mix[a2]])

                if MIX_LEVEL >= 4:
                    if n + 1 < NT:
                        swa_fb[n + 1][0]()
                    swa_fb[n][2]()
                    if n + 1 < NT:
                        tokmajor(n + 1)
                    swa_fb[n][3]()
                    if n + 1 < NT:
                        swa_fb[n + 1][1]()
            def mix_T(n):
                a2 = n % 2
                for k in range(8):
                    sc.op("pe", lambda e, k=k: e.transpose(bankb(6)[:, k * 128:(k + 1) * 128], mixt[:, a2, k * 128:(k + 1) * 128], ident[:]), reads=[t_mix[a2], t_cst], writes=[t_bank[6]])
                sc.op("act", lambda e: e.copy(out=aT[:, abuf_out, :, n * 128:(n + 1) * 128], in_=bankb(6).rearrange("p (k c) -> p k c", k=8)), reads=[t_bank[6]], writes=[t_aT[abuf_out][n]])

            for n in range(NT):
                tile_body(n)
            if MIX_LEVEL >= 5:
                mix_T(NT - 1)
            if MIX_LEVEL < 5:
                skip_units(4)
                return
            ous = [take_unit() for _ in range(4)]
            for i in range(NT):
                bk = 4 + 2 * (i % 2)
                tok_major_mm(i, ous, 2, 1024, abuf_out, bk)
                if i == NT - 1:
                    release_units(ous)
                if next_pre is not None and i >= 2:
                    next_pre(i - 2)
                postnorm_add(i, bk, 1)
            if next_pre is not None:
                next_pre(NT - 2)
                next_pre(NT - 1)

        def ple_prep(blk):
            t0 = blk * TB
            for i in range(NT):
                a2 = i % 2
                sc.dma("sp", ptok[:, a2, :], pin[t0 + i * 128:t0 + (i + 1) * 128, :], writes=[t_ptok[a2]], pool=("pl", 2))
                sc.op("pool", lambda e, a2=a2: e.tensor_copy(out=pbf[:, a2, :], in_=ptok[:, a2, :]), reads=[t_ptok[a2]], writes=[t_pbf[a2]])
                for k in range(2):
                    sc.op("pe", lambda e, k=k, a2=a2: e.transpose(bankb(3)[:, k * 128:(k + 1) * 128], pbf[:, a2, k * 128:(k + 1) * 128], ident[:]), reads=[t_pbf[a2], t_cst], writes=[t_bank[3]])
                sc.op("act", lambda e, i=i: e.copy(out=pT[:, :, i * 128:(i + 1) * 128], in_=bankb(3)[:, 0:256].rearrange("p (k c) -> p k c", k=2)), reads=[t_bank[3]], writes=[t_pT[i]])

        def ple(blk, abuf, do_pre=True, after_tile=None):
            gus = [take_unit() for _ in range(4)]
            pu = [take_unit()]
            if do_pre:
                for i in range(NT):
                    prenorm_T(i, 0, abuf, do_norm=False)
            for i in range(NT):
                ti = i % 2
                tok_major_mm(i, gus, 2, 1024, abuf, 4)
                for half in range(2):
                    for k in range(2):
                        s = uslot(pu[0])
                        sc.op("pe", lambda e, k=k, half=half, s=s, i=i: e.matmul(bank(6 + half), lhsT=pT[:, k, i * 128:(i + 1) * 128], rhs=ring[:, s, k * 1024 + half * 512:k * 1024 + half * 512 + 512], start=(k == 0), stop=(k == 1)),
                              reads=[t_ring[s], t_pT[i]], writes=[t_bank[6 + half]])
                if i == NT - 1:
                    release_units(gus + pu)
                sc.op("act", lambda e, ti=ti: e.activation(out=tmp[:, ti, :], in_=psum[2][:, :], func=AF.Tanh, scale=0.5), reads=[t_bank[4], t_bank[5]], writes=[t_tmp[ti]])
                sc.op("dve", lambda e, ti=ti: e.scalar_tensor_tensor(out=tmp[:, ti, :], in0=tmp[:, ti, :], scalar=1.0, in1=psum[3][:, :], op0=ALU.add, op1=ALU.mult),
                      reads=[t_tmp[ti], t_bank[6], t_bank[7]], writes=[t_tmp[ti]])
                ms, tms = new_ms()
                r, tr = new_r()
                sc.op("act", lambda e, ti=ti, ms=ms: e.activation(out=xn[:, ti, :], in_=tmp[:, ti, :], func=AF.Square, scale=1.0 / 32.0, accum_out=ms[:, 0:1]), reads=[t_tmp[ti]], writes=[t_xn[ti], tms])
                sc.op("pool", lambda e, ms=ms, r=r: e.tensor_scalar(out=r[:, 0:1], in0=ms[:, 0:1], scalar1=4.0 * EPS, scalar2=None, op0=ALU.add), reads=[tms], writes=[tr])
                sc.op("pool", lambda e, r=r: e.tensor_tensor(out=r[:, 1:2], in0=r[:, 0:1], in1=cst[:, C_NHALF:C_NHALF + 1], op=ALU.pow), reads=[tr, t_cst], writes=[tr])
                gb = bcs[:, B_GPOST + 3 * 1024:B_GPOST + 4 * 1024]
                sc.op("dve", lambda e, ti=ti, r=r: e.scalar_tensor_tensor(out=tmp[:, ti, :], in0=tmp[:, ti, :], scalar=r[:, 1:2], in1=gb, op0=ALU.mult, op1=ALU.mult), reads=[t_tmp[ti], tr, t_cst], writes=[t_tmp[ti]])
                sc.op("pool", lambda e, i=i, ti=ti: e.tensor_tensor(out=h[:, i, :], in0=h[:, i, :], in1=tmp[:, ti, :], op=ALU.add), reads=[t_tmp[ti]], writes=[t_h[i]])
                if after_tile is not None:
                    after_tile(i)

        for blk in range(NBLK_RUN):
            t0 = blk * TB
            full = STOP_AFTER >= 4
            if blk == 0 or not full:
                for i in range(NT):
                    sc.dma("sp", h[:, i, :], x[t0 + i * 128:t0 + (i + 1) * 128, :], writes=[t_h[i]], pool=("xl", 4))
            if blk == 0:
                for g in range(RING):
                    load_unit(g)
            PIPE = True
            ffn(0, 0, 0, next_pre=(lambda i: prenorm_T(i, 1, 1)) if (PIPE and STOP_AFTER >= 2) else None)
            if STOP_AFTER >= 2:
                mixer(blk, 1, 0, do_pre=not PIPE, next_pre=(lambda i: prenorm_T(i, 2, 1)) if (PIPE and STOP_AFTER >= 3 and MIX_LEVEL >= 5) else None)
            else:
                skip_units(16)
            if STOP_AFTER >= 4:
                ple_prep(blk)
            if STOP_AFTER >= 3:
                ffn(2, 2, 1, do_pre=not (PIPE and MIX_LEVEL >= 5), next_pre=(lambda i: prenorm_T(i, 0, 0, do_norm=False)) if (PIPE and STOP_AFTER >= 4) else None)
            else:
                skip_units(33)
            def store_load(i, blk=blk, t0=t0):
                sc.dma("sp", y[t0 + i * 128:t0 + (i + 1) * 128, :], h[:, i, :], reads=[t_h[i]], pool=("st", 4))
                if blk + 1 < NBLK_RUN:
                    t1 = t0 + TB
                    sc.dma("sp", h[:, i, :], x[t1 + i * 128:t1 + (i + 1) * 128, :], writes=[t_h[i]], pool=("xl", 4))

            if full:
                ple(blk, 0, do_pre=not PIPE, after_tile=store_load)
            else:
                skip_units(5)
                for i in range(NT):
                    sc.dma("sp", y[t0 + i * 128:t0 + (i + 1) * 128, :], h[:, i, :], reads=[t_h[i]], pool=("st", 4))
        sc.emit(st)
    return nc


def _fpair(W, cols_a, cols_b):
    out = np.empty((128, 2, 8, 128), np.float32)
    for s_, cols in enumerate((cols_a, cols_b)):
        out[:, s_] = W[:, cols].reshape(8, 128, 128).transpose(1, 0, 2)
    return out.reshape(128, UE)


def _tunit(W, k0, kk, n0, n):
    blk = W[k0 * 128:(k0 + kk) * 128, n0:n0 + n].reshape(kk, 128, n).transpose(1, 0, 2).reshape(128, kk * n)
    out = np.zeros((128, UE), np.float32)
    out[:, :kk * n] = blk
    return out


def _ffn_units(wg, wu_, wd):
    us = []
    for j in range(22):
        c = np.arange(j * 128, (j + 1) * 128)
        us.append(_fpair(np.concatenate([wg, wu_], axis=1), c, c + DFF))
    for m in range(11):
        us.append(_tunit(wd, 2 * m, 2, 0, 1024))
    return us


def _swap_cols(base):
    cols = []
    for hh in range(4):
        cols += list(range(base + hh * 64 + 32, base + hh * 64 + 64)) + list(range(base + hh * 64, base + hh * 64 + 32))
    return np.array(cols)


def _win_chunks():
    rq = np.arange(0, 256)
    rqs = _swap_cols(0)
    rk = np.arange(256, 512)
    rks = _swap_cols(256)
    sq = np.arange(1536, 2048)
    sk0 = np.concatenate([np.arange(2048, 2112)] * 2)
    sk1 = np.concatenate([np.arange(2112, 2176)] * 2)
    pairs = [(rq[0:128], rqs[0:128]), (rq[128:256], rqs[128:256]), (rk[0:128], rks[0:128]), (rk[128:256], rks[128:256]),
             (sq[0:128], sq[128:256]), (sq[256:384], sq[384:512]), (sk0, sk1)]
    return pairs


def _host_prep(inp):
    f = lambda k: np.asarray(inp[k], np.float32)[0]
    units = []
    units += _ffn_units(f("ffn1_w_gate"), f("ffn1_w_up"), f("ffn1_w_down"))
    w_in = f("w_in")
    pairs = _win_chunks()
    for a, b in pairs:
        units.append(_fpair(w_in, a, b))
    units.append(_tunit(w_in, 0, 4, 512, 512))
    units.append(_tunit(w_in, 4, 4, 512, 512))
    units.append(_tunit(w_in, 0, 4, 1024, 512))
    units.append(_tunit(w_in, 4, 4, 1024, 512))
    units.append(_tunit(w_in, 0, 8, 2176, 128))
    w_out = f("w_out")
    rowperm = list(range(512))
    for s_ in range(8):
        hd = (s_ // 4) * 4 + PERM[s_ % 4]
        rowperm += list(range(512 + hd * 64, 512 + (hd + 1) * 64))
    w_out = w_out[np.array(rowperm)]
    for m in range(4):
        units.append(_tunit(w_out, 2 * m, 2, 0, 1024))
    units += _ffn_units(f("ffn2_w_gate"), f("ffn2_w_up"), f("ffn2_w_down"))
    pg = f("ple_w_gate")
    for m in range(4):
        units.append(_tunit(pg, 2 * m, 2, 0, 1024))
    units.append(_tunit(f("ple_w_proj"), 0, 2, 0, 1024))
    assert len(units) == NU
    wu = np.ascontiguousarray(np.stack(units).reshape(NU * 128, UE))

    cst = np.zeros((128, NC_), np.float32)
    for s_, name in enumerate(("ffn1_pre_g", "mix_pre_g", "ffn2_pre_g")):
        cst[:, C_GPRE + s_ * 8:C_GPRE + s_ * 8 + 8] = f(name).reshape(8, 128).T
    b_in = f("b_in")
    bcols = []
    for a, b in pairs[0:4]:
        bcols += [b_in[a], b_in[b]]
    for a, b in pairs[4:7]:
        bcols += [b_in[a], b_in[b]]
    cst[:, C_BIASF:C_BIASF + 14] = np.stack(bcols, axis=1)
    pidx = np.arange(128)
    cst[:, C_INVF] = (10000.0 ** (-(pidx % 32).astype(np.float32) / 32.0)).astype(np.float32)
    sgn = np.where((pidx % 64) < 32, -1.0, 1.0).astype(np.float32)
    cst[:, C_NSGN] = sgn
    cst[:, C_SGNPI] = sgn * np.float32(np.pi)
    hidx = np.arange(4, dtype=np.float64)
    lg = np.log(1.0 - 2.0 ** (-5.0 - hidx))
    for hp in range(2):
        hsel = hp * 2 + pidx // 64
        cst[:, C_GAMC + hp] = np.exp(lg[hsel] * 128.0)
        cc = np.arange(128, dtype=np.float64)
        cst[:, C_QDEC + hp * 128:C_QDEC + (hp + 1) * 128] = np.exp(lg[hsel][:, None] * (cc[None, :] + 1.0))
        cst[:, C_KDEC + hp * 128:C_KDEC + (hp + 1) * 128] = 0.125 * np.exp(-lg[hsel][:, None] * (cc[None, :] + 1.0))
    m = np.arange(128)
    cst[:, C_CAUS:C_CAUS + 128] = (m[None, :] >= m[:, None]).astype(np.float32)
    q = np.arange(128)[:, None]
    kk = np.arange(256)[None, :]
    rel = (q + 128) - kk
    valid = (rel >= 0) & (rel < 128)
    cst[:, C_SWAM:C_SWAM + 256] = np.where(valid, 0.0, NEG)
    cst[:, C_SWAM0:C_SWAM0 + 256] = np.where(valid & (kk >= 128), 0.0, NEG)
    cst[:, C_NHALF] = -0.5
    cst[:, C_PI] = np.float32(np.pi)

    bc = np.zeros((128, NB_), np.float32)
    for gi, name in enumerate(("ffn1_post_g", "mix_post_g", "ffn2_post_g", "ple_norm_g")):
        bc[:, B_GPOST + gi * 1024:B_GPOST + (gi + 1) * 1024] = f(name)[None, :]
    bc[:, B_BIAS:B_BIAS + 512] = b_in[512:1024][None, :]
    bc[:, B_BIAS + 512:B_BIAS + 1024] = b_in[1024:1536][None, :]
    bc[:, B_BIAS + 1024:B_BIAS + 1152] = b_in[2176:2304][None, :]
    sk_perm = [r * 4 + PERM[j] for r in range(2) for j in range(4)]
    bc[:, B_SINK:B_SINK + 8] = f("swa_sinks")[sk_perm][None, :]
    ident = np.eye(128, dtype=np.float32).astype(ml_dtypes.bfloat16)
    return wu, cst, bc, ident


_CACHE = {}


def kernel(**inputs):
    wu, cst, bc, ident = _host_prep(inputs)
    x = np.asarray(inputs["x"], np.float32)
    p = np.asarray(inputs["p"], np.float32)[0]
    pos = np.asarray(inputs["positions"], np.int32)
    if "nc" not in _CACHE:
        _CACHE["nc"] = build_program()
    nc = _CACHE["nc"]
    in_maps = []
    for b in range(8):
        in_maps.append({"x": np.ascontiguousarray(x[b]), "p": np.ascontiguousarray(p[b]), "pos": np.ascontiguousarray(pos[b:b + 1]),
                        "wu": wu, "cst": cst, "bc": bc, "ident": ident})
    res = run_bass_kernel_spmd(nc, in_maps, core_ids=list(range(8)))
    out = np.stack([np.asarray(r["y"], np.float32) for r in res.results], axis=0)
    return out
```

```python
import numpy as np
import ml_dtypes
from contextlib import ExitStack
import concourse.bass as bass
import concourse.mybir as mybir
from concourse.bass_utils import run_bass_kernel_spmd

F32, BF16, I32 = mybir.dt.float32, mybir.dt.bfloat16, mybir.dt.int32
AF = mybir.ActivationFunctionType
ALU = mybir.AluOpType
AX = mybir.AxisListType

D = 1024
S = 4096
DFF = 2816
TB = 512
NT = TB // 128
NBLK = S // TB
RING = 16
UE = 2048
EPS = 1e-6
NU_FFN = 33
NU = 87
PRE_G = 1
NEG = -30000.0
STOP_AFTER = 99
NBLK_RUN = NBLK
MIX_LEVEL = 5
MIX_SUB = 9
PERM = (0, 2, 1, 3)

C_GPRE = 0
C_BIASF = 24
C_INVF = 38
C_NSGN = 39
C_SGNPI = 40
C_GAMC = 41
C_CAUS = 43
C_SWAM = 171
C_SWAM0 = 427
C_QDEC = 683
C_KDEC = 939
C_NHALF = 1195
C_PI = 1196
NC_ = 1197
B_GPOST = 0
B_BIAS = 4096
B_SINK = 5248
NB_ = 5256


class T:
    __slots__ = ("name", "w", "r", "excl")

    def __init__(self, name, excl=False):
        self.name = name
        self.w = None
        self.r = {}
        self.excl = excl


class Sched:
    ENG = ("pe", "act", "dve", "pool", "sp")

    def __init__(self, nc):
        self.nc = nc
        self.ops = {e: [] for e in self.ENG}
        self.cnt = {e: 0 for e in self.ENG}
        self.seen = {e: {} for e in self.ENG}
        self.dcnt = {}
        self.rot = {}

    def _needs(self, reads, writes):
        need = {}

        def req(k, v):
            if need.get(k, 0) < v:
                need[k] = v
        W = list(writes)
        Rd = []
        for t in reads:
            (W if t.excl else Rd).append(t)
        for t in Rd:
            if t.w is not None:
                req(*t.w)
        for t in W:
            if t.w is not None:
                req(*t.w)
            for k, v in t.r.items():
                req(k, v)
        return need, Rd, W

    def op(self, eng, fn, reads=(), writes=()):
        need, Rd, W = self._needs(reads, writes)
        seq = self.cnt[eng] + 1
        waits = []
        for k, v in need.items():
            if k == eng and (eng == "pe" or seq - v > 3):
                continue
            if self.seen[eng].get(k, 0) >= v:
                continue
            self.seen[eng][k] = v
            waits.append((k, v))
        self.cnt[eng] = seq
        self.ops[eng].append((waits, fn, False))
        for t in Rd:
            if t.r.get(eng, 0) < seq:
                t.r[eng] = seq
        for t in W:
            t.w = (eng, seq)
            t.r = {}

    def dma(self, q, out, in_, reads=(), writes=(), sem=None, pool=("d", 4)):
        if sem is None:
            name, n = pool
            i = self.rot.get(name, 0)
            self.rot[name] = i + 1
            sem = f"{name}{i % n}"
        need, Rd, W = self._needs(reads, writes)
        prev = self.dcnt.get(sem, 0)
        if prev > 0 and need.get(sem, 0) < prev:
            need[sem] = prev
        val = prev + 16
        self.dcnt[sem] = val
        waits = []
        for k, v in need.items():
            if self.seen[q].get(k, 0) >= v:
                continue
            self.seen[q][k] = v
            waits.append((k, v))
        self.ops[q].append((waits, (out, in_, sem), True))
        for t in Rd:
            if t.r.get(sem, 0) < val:
                t.r[sem] = val
        for t in W:
            t.w = (sem, val)
            t.r = {}

    def emit(self, stack):
        nc = self.nc
        ENG = self.ENG
        waited = {e: set() for e in ENG}
        for e in ENG:
            for waits, fn, isdma in self.ops[e]:
                for k, v in waits:
                    if k in ENG:
                        waited[k].add(v)
        rank = {e: {v: i + 1 for i, v in enumerate(sorted(waited[e]))} for e in ENG}
        semh = {}
        for e in ENG:
            semh[e] = stack.enter_context(nc.semaphore("E_" + e))
        for k in self.dcnt:
            semh[k] = stack.enter_context(nc.semaphore("D_" + k))
        fin = [(k, v) for k, v in self.dcnt.items()]
        self.ops["sp"].append((fin, None, False))

        def run(ename, eobj):
            seq = 0
            for waits, fn, isdma in self.ops[ename]:
                for k, v in waits:
                    eobj.wait_ge(semh[k], rank[k][v] if k in ENG else v)
                if fn is None:
                    continue
                if isdma:
                    out, in_, sem = fn
                    eobj.dma_start(out=out, in_=in_).then_inc(semh[sem], 16)
                else:
                    ins = fn(eobj)
                    seq += 1
                    if seq in rank[ename]:
                        ins.then_inc(semh[ename], 1)

        with nc.Block() as block:
            block.tensor(lambda e: run("pe", e))
            block.scalar(lambda e: run("act", e))
            block.vector(lambda e: run("dve", e))
            block.gpsimd(lambda e: run("pool", e))
            block.sync(lambda e: run("sp", e))


def build_program():
    nc = bass.Bass("TRN2", target_bir_lowering=False)
    x = nc.dram_tensor("x", [S, D], F32, kind="ExternalInput").ap()
    pin = nc.dram_tensor("p", [S, 256], F32, kind="ExternalInput").ap()
    pos = nc.dram_tensor("pos", [1, S], I32, kind="ExternalInput").ap()
    wu = nc.dram_tensor("wu", [NU * 128, UE], F32, kind="ExternalInput").ap()
    cst_d = nc.dram_tensor("cst", [128, NC_], F32, kind="ExternalInput").ap()
    bc_d = nc.dram_tensor("bc", [128, NB_], F32, kind="ExternalInput").ap()
    ident_d = nc.dram_tensor("ident", [128, 128], BF16, kind="ExternalInput").ap()
    y = nc.dram_tensor("y", [S, D], F32, kind="ExternalOutput").ap()
    scr = nc.dram_tensor("scr", [NU * 128, UE], BF16, kind="Internal").ap()

    sc = Sched(nc)
    with ExitStack() as st:
        def sb(name, shape, dt):
            return st.enter_context(nc.sbuf_tensor("s_" + name, shape, dt))

        h = sb("h", [128, NT, D], F32)
        aT = sb("aT", [128, 2, 8, TB], BF16)
        hT = sb("hT", [128, 22, TB], BF16)
        ring = sb("ring", [128, RING, UE], BF16)
        cst = sb("cst", [128, NC_], F32)
        bcs = sb("bcs", [128, NB_], F32)
        ident = sb("ident", [128, 128], BF16)
        xn = sb("xn", [128, 2, D], BF16)
        tmp = sb("tmp", [128, 2, D], F32)
        sg = sb("sg", [128, 2, TB], F32)
        st_ms = sb("st_ms", [128, 16, 2], F32)
        st_r = sb("st_r", [128, 16, 2], F32)
        qdT = sb("qdT", [128, 2, TB], BF16)
        kiT = sb("kiT", [128, 2, TB], BF16)
        sqT = sb("sqT", [128, 4, TB], BF16)
        skT = sb("skT", [128, 2, 2 * TB], BF16)
        svt = sb("svt", [128, 8, 128], BF16)
        vtok = sb("vtok", [128, 2, 512], BF16)
        rgs = sb("rgs", [128, 2, 512], F32)
        kitok = sb("kitok", [128, 2, 256], BF16)
        sTm = sb("sTm", [128, 2, 512], BF16)
        stU = sb("stU", [128, 2, 128], F32)
        stB = sb("stB", [128, 2, 2, 128], BF16)
        gst = sb("gst", [128, 4, 4], F32)
        swS = sb("swS", [128, 8, 256], F32)
        swP = sb("swP", [128, 8, 256], BF16)
        swPT = sb("swPT", [128, 16, 128], BF16)
        ang = swS[:, 0:2, :].rearrange("p a b -> p (a b)")
        cos2 = swS[:, 2:4, :].rearrange("p a b -> p (a b)")
        sin2 = swS[:, 4:6, :].rearrange("p a b -> p (a b)")
        posi = ang.bitcast(I32)
        swst = sb("swst", [128, 2, 8, 8], F32)
        ptok = sb("ptok", [128, 2, 256], F32)
        pbf = sb("pbf", [128, 2, 256], BF16)
        pT = sb("pT", [128, 2, TB], BF16)
        psum = [st.enter_context(nc.psum_tensor(f"ps{i}", [128, 1024], F32)) for i in range(4)]

        rA, rB = tmp[:, 0, 0:512], tmp[:, 1, 0:512]
        osb, osq = tmp[:, 0, 512:1024], tmp[:, 1, 512:1024]
        mixt = xn
        t_h = [T(f"h{i}") for i in range(NT)]
        t_aT = [[T(f"aT{a}_{i}") for i in range(NT)] for a in range(2)]
        t_hT = [T(f"hT{j}") for j in range(22)]
        t_ring = [T(f"ring{s}") for s in range(RING)]
        t_scr = [T(f"scr{g}") for g in range((NU + PRE_G - 1) // PRE_G)]
        t_cst = T("cst")
        t_xn = [T("xn0"), T("xn1")]
        t_tmp = [T("tmp0"), T("tmp1")]
        t_sg = [T("sg0"), T("sg1")]
        t_ms = [T(f"ms{i}") for i in range(16)]
        t_r = [T(f"r{i}") for i in range(16)]
        t_bank = [T(f"bank{i}", excl=True) for i in range(8)]
        t_rot = None
        t_qd = [T("qd0"), T("qd1")]
        t_ki = [T("ki0"), T("ki1")]
        t_sq = [T(f"sq{c}") for c in range(4)]
        t_sk = [[T(f"sk{kv}_{g}") for g in range(8)] for kv in range(2)]
        t_sv = [T(f"sv{g}") for g in range(8)]
        t_v = [T("v0"), T("v1")]
        t_rg = [T("rg0"), T("rg1")]
        t_kit = [T("kit0"), T("kit1")]
        t_sTm = [T("sTm0"), T("sTm1")]
        t_U = [T("U0"), T("U1")]
        t_sB = [[T(f"sB{a}_{hp}") for hp in range(2)] for a in range(2)]
        t_osb = t_tmp[0]
        t_osq = t_tmp[1]
        t_gst = T("gst")
        t_swS = T("swS")
        t_rot = t_swS
        t_swP = T("swP")
        t_swPT = T("swPT")
        t_swst = [T("swst0"), T("swst1")]
        t_mix = t_xn
        t_ptok = [T("ptok0"), T("ptok1")]
        t_pbf = [T("pbf0"), T("pbf1")]
        t_pT = [T(f"pT{i}") for i in range(NT)]

        def bank(i):
            return psum[i // 2][:, (i % 2) * 512:(i % 2) * 512 + 512]

        def bankb(i):
            return bank(i).bitcast(BF16)

        cnt = {"ms": 0, "r": 0}

        def new_ms():
            i = cnt["ms"] % 16
            cnt["ms"] += 1
            return st_ms[:, i, :], t_ms[i]

        def new_r():
            i = cnt["r"] % 16
            cnt["r"] += 1
            return st_r[:, i, :], t_r[i]

        sc.dma("sp", cst[:], cst_d, writes=[t_cst], sem="c0")
        sc.dma("sp", bcs[:], bc_d, writes=[t_cst], sem="c1")
        sc.dma("sp", ident[:], ident_d, writes=[t_cst], sem="c2")
        for hp in range(2):
            sc.op("pool", lambda e, hp=hp: e.memset(stU[:, hp, :], 0.0), writes=[t_U[hp]])
            sc.op("pool", lambda e, hp=hp: e.memset(stB[:, 0, hp, :], 0.0), writes=[t_sB[0][hp]])
            sc.op("pool", lambda e, hp=hp: e.memset(stB[:, 1, hp, :], 0.0), writes=[t_sB[1][hp]])
        for kv in range(2):
            sc.op("pool", lambda e, kv=kv: e.memset(skT[:, kv, :], 0.0), writes=t_sk[kv])
        sc.op("pool", lambda e: e.memset(svt[:].rearrange("p a b -> p (a b)"), 0.0), writes=t_sv)
        pre = {"next": 0}
        NGRP = len(t_scr)

        def prepass_upto(q):
            q = min(q, NGRP - 1)
            while pre["next"] <= q:
                g = pre["next"]
                pre["next"] += 1
                r0 = g * PRE_G * 128
                r1 = min(NU, (g + 1) * PRE_G) * 128
                sc.dma("pool", scr[r0:r1, :], wu[r0:r1, :], reads=(t_h if g == 0 else ()), writes=[t_scr[g]], pool=("pp", 10))

        wst = {"next_load": 0, "next_use": 0}
        total_units = NU * NBLK_RUN

        def load_unit(g):
            if g >= total_units:
                return
            u = g % NU
            s = g % RING
            if g < NU:
                prepass_upto(u // PRE_G + 6)
            sc.dma("sp", ring[:, s, :], scr[u * 128:(u + 1) * 128, :], reads=[t_scr[u // PRE_G]], writes=[t_ring[s]], sem=f"w{s}")


        def take_unit():
            g = wst["next_use"]
            wst["next_use"] += 1
            return g

        def release_units(gs):
            for g in gs:
                load_unit(g + RING)

        def uslot(g):
            return g % RING

        def skip_units(n):
            gs = [take_unit() for _ in range(n)]
            release_units(gs)

        def prenorm_T(i, site, abuf, do_norm=True):
            xi = i % 2
            if do_norm:
                ms, tms = new_ms()
                r, tr = new_r()
                sc.op("act", lambda e: e.activation(out=xn[:, xi, :], in_=h[:, i, :], func=AF.Square, scale=1.0 / 32.0, accum_out=ms[:, 0:1]),
                      reads=[t_h[i]], writes=[t_xn[xi], tms])
                sc.op("pool", lambda e: e.tensor_scalar(out=r[:, 0:1], in0=ms[:, 0:1], scalar1=EPS, scalar2=None, op0=ALU.add), reads=[tms], writes=[tr])
                sc.op("pool", lambda e: e.tensor_tensor(out=r[:, 1:2], in0=r[:, 0:1], in1=cst[:, C_NHALF:C_NHALF + 1], op=ALU.pow), reads=[tr, t_cst], writes=[tr])
                sc.op("dve", lambda e: e.tensor_scalar(out=xn[:, xi, :], in0=h[:, i, :], scalar1=r[:, 1:2], scalar2=None, op0=ALU.mult),
                      reads=[t_h[i], tr], writes=[t_xn[xi]])
            else:
                sc.op("dve", lambda e: e.tensor_copy(out=xn[:, xi, :], in_=h[:, i, :]), reads=[t_h[i]], writes=[t_xn[xi]])
            b = i % 4
            for k in range(8):
                sc.op("pe", lambda e, k=k: e.transpose(bankb(b)[:, k * 128:(k + 1) * 128], xn[:, xi, k * 128:(k + 1) * 128], ident[:]),
                      reads=[t_xn[xi], t_cst], writes=[t_bank[b]])
            dst = aT[:, abuf, :, i * 128:(i + 1) * 128]
            src = bankb(b).rearrange("p (k c) -> p k c", k=8)
            if do_norm:
                gp = cst[:, C_GPRE + site * 8:C_GPRE + site * 8 + 8].unsqueeze(2).broadcast_to([128, 8, 128])
                sc.op("dve", lambda e: e.tensor_tensor(out=dst, in0=src, in1=gp, op=ALU.mult), reads=[t_bank[b], t_cst], writes=[t_aT[abuf][i]])
            else:
                sc.op("act", lambda e: e.copy(out=dst, in_=src), reads=[t_bank[b]], writes=[t_aT[abuf][i]])

        def postnorm_add(i, bk, gidx, src_sb=None, half=False):
            ti = i % 2
            sq_scale = (2.0 if half else 1.0) / 32.0
            eps_v = (4.0 if half else 1.0) * EPS
            ms, tms = new_ms()
            r, tr = new_r()
            if src_sb is None:
                f = psum[bk // 2][:, :]
                rd = [t_bank[bk], t_bank[bk + 1]]
                for hh in range(2):
                    sc.op("act", lambda e, hh=hh: e.activation(out=xn[:, ti, hh * 512:(hh + 1) * 512], in_=bank(bk + hh), func=AF.Square, scale=sq_scale, accum_out=ms[:, hh:hh + 1]),
                          reads=[t_bank[bk + hh]], writes=[t_xn[ti], tms])
                sc.op("pool", lambda e: e.tensor_tensor(out=r[:, 0:1], in0=ms[:, 0:1], in1=ms[:, 1:2], op=ALU.add), reads=[tms], writes=[tr])
                sc.op("pool", lambda e: e.tensor_scalar(out=r[:, 0:1], in0=r[:, 0:1], scalar1=eps_v, scalar2=None, op0=ALU.add), reads=[tr], writes=[tr])
            else:
                f, tsrc = src_sb
                rd = [tsrc]
                sc.op("act", lambda e: e.activation(out=xn[:, ti, :], in_=f, func=AF.Square, scale=sq_scale, accum_out=ms[:, 0:1]),
                      reads=[tsrc], writes=[t_xn[ti], tms])
                sc.op("pool", lambda e: e.tensor_scalar(out=r[:, 0:1], in0=ms[:, 0:1], scalar1=eps_v, scalar2=None, op0=ALU.add), reads=[tms], writes=[tr])
            sc.op("pool", lambda e: e.tensor_tensor(out=r[:, 1:2], in0=r[:, 0:1], in1=cst[:, C_NHALF:C_NHALF + 1], op=ALU.pow), reads=[tr, t_cst], writes=[tr])
            gb = bcs[:, B_GPOST + gidx * 1024:B_GPOST + (gidx + 1) * 1024]
            sc.op("dve", lambda e: e.scalar_tensor_tensor(out=tmp[:, ti, :], in0=f, scalar=r[:, 1:2], in1=gb, op0=ALU.mult, op1=ALU.mult),
                  reads=rd + [tr, t_cst], writes=[t_tmp[ti]])
            sc.op("dve", lambda e: e.tensor_tensor(out=h[:, i, :], in0=h[:, i, :], in1=tmp[:, ti, :], op=ALU.add), reads=[t_tmp[ti]], writes=[t_h[i]])

        def ffn(site_pre, gpost, abuf, do_pre=True, next_pre=None):
            if do_pre:
                for i in range(NT):
                    prenorm_T(i, site_pre, abuf)
            rd_a = t_aT[abuf]
            gus = []
            for j in range(22):
                g = take_unit()
                s = uslot(g)
                bg, bu = (0, 1) if j % 2 == 0 else (2, 3)
                for half, bb in ((0, bg), (1, bu)):
                    for k in range(8):
                        sc.op("pe", lambda e, k=k, half=half, bb=bb, s=s: e.matmul(bank(bb), lhsT=ring[:, s, half * 1024 + k * 128: half * 1024 + (k + 1) * 128],
                                                                                   rhs=aT[:, abuf, k, :], start=(k == 0), stop=(k == 7)),
                              reads=[t_ring[s]] + rd_a, writes=[t_bank[bb]])
                release_units([g])
                sj = j % 2
                sc.op("act", lambda e, bg=bg, sj=sj: e.activation(out=sg[:, sj, :], in_=bank(bg), func=AF.Silu), reads=[t_bank[bg]], writes=[t_sg[sj]])
                sc.op("dve", lambda e, bu=bu, sj=sj, j=j: e.tensor_tensor(out=hT[:, j, :], in0=bank(bu), in1=sg[:, sj, :], op=ALU.mult),
                      reads=[t_bank[bu], t_sg[sj]], writes=[t_hT[j]])
            dus = [take_unit() for _ in range(11)]
            for i in range(NT):
                bk = 4 + 2 * (i % 2)
                for half in range(2):
                    for k in range(22):
                        s = uslot(dus[k // 2])
                        sc.op("pe", lambda e, k=k, half=half, s=s, i=i, bk=bk: e.matmul(bank(bk + half), lhsT=hT[:, k, i * 128:(i + 1) * 128],
                                                                                         rhs=ring[:, s, (k % 2) * 1024 + half * 512:(k % 2) * 1024 + half * 512 + 512],
                                                                                         start=(k == 0), stop=(k == 21)),
                              reads=[t_ring[s], t_hT[k]], writes=[t_bank[bk + half]])
                if i == NT - 1:
                    release_units(dus)
                if next_pre is not None and i >= 2:
                    next_pre(i - 2)
                postnorm_add(i, bk, gpost, half=True)
            if next_pre is not None:
                next_pre(NT - 2)
                next_pre(NT - 1)

        def tok_major_mm(i, units, kk, n, abuf, bk, nk=8):
            for half in range((n + 511) // 512):
                w = min(512, n - half * 512)
                for k in range(nk):
                    s = uslot(units[k // kk])
                    off = (k % kk) * n + half * 512
                    sc.op("pe", lambda e, k=k, s=s, off=off, w=w, half=half: e.matmul(bank(bk + half)[:, 0:w], lhsT=aT[:, abuf, k, i * 128:(i + 1) * 128],
                                                                                       rhs=ring[:, s, off:off + w], start=(k == 0), stop=(k == nk - 1)),
                          reads=[t_ring[s], t_aT[abuf][i]], writes=[t_bank[bk + half]])

        def mixer(blk, abuf_in, abuf_out, do_pre=True, next_pre=None):
            t0 = blk * TB
            if do_pre:
                for i in range(NT):
                    prenorm_T(i, 1, abuf_in)
            rd_a = t_aT[abuf_in]
            sc.dma("sp", posi[:], pos[:, t0:t0 + TB].broadcast_to([128, TB]), writes=[t_rot], pool=("m", 2))
            sc.op("dve", lambda e: e.tensor_copy(out=ang[:], in_=posi[:]), reads=[t_rot], writes=[t_rot])
            sc.op("dve", lambda e: e.tensor_scalar(out=ang[:], in0=ang[:], scalar1=cst[:, C_INVF:C_INVF + 1], scalar2=None, op0=ALU.mult), reads=[t_rot, t_cst], writes=[t_rot])
            TWO_PI = 2.0 * np.pi
            C1 = 6.28125
            C2 = TWO_PI - C1
            kf = rB
            ki = rA.bitcast(I32)

            def range_reduce(dst, shift):
                sc.op("dve", lambda e: e.tensor_scalar(out=dst[:], in0=ang[:], scalar1=shift, scalar2=1.0 / TWO_PI, op0=ALU.add, op1=ALU.mult), reads=[t_rot], writes=[t_rot])
                sc.op("dve", lambda e: e.tensor_copy(out=ki, in_=dst[:]), reads=[t_rot], writes=[t_tmp[0]])
                sc.op("dve", lambda e: e.tensor_copy(out=kf, in_=ki), reads=[t_tmp[0]], writes=[t_tmp[1]])
                sc.op("dve", lambda e: e.scalar_tensor_tensor(out=dst[:], in0=kf, scalar=-C1, in1=ang[:], op0=ALU.mult, op1=ALU.add), reads=[t_tmp[1], t_rot], writes=[t_rot])
                sc.op("dve", lambda e: e.scalar_tensor_tensor(out=dst[:], in0=kf, scalar=-C2, in1=dst[:], op0=ALU.mult, op1=ALU.add), reads=[t_tmp[1], t_rot], writes=[t_rot])
                if shift != 0.0:
                    sc.op("dve", lambda e: e.tensor_scalar(out=dst[:], in0=dst[:], scalar1=shift, scalar2=None, op0=ALU.add), reads=[t_rot], writes=[t_rot])
                sc.op("dve", lambda e: e.tensor_scalar(out=kf, in0=dst[:], scalar1=float(np.pi), scalar2=-TWO_PI, op0=ALU.is_gt, op1=ALU.mult), reads=[t_rot], writes=[t_tmp[1]])
                sc.op("dve", lambda e: e.tensor_tensor(out=dst[:], in0=dst[:], in1=kf, op=ALU.add), reads=[t_rot, t_tmp[1]], writes=[t_rot])
                sc.op("dve", lambda e: e.tensor_scalar(out=kf, in0=dst[:], scalar1=-float(np.pi), scalar2=TWO_PI, op0=ALU.is_lt, op1=ALU.mult), reads=[t_rot], writes=[t_tmp[1]])
                sc.op("dve", lambda e: e.tensor_tensor(out=dst[:], in0=dst[:], in1=kf, op=ALU.add), reads=[t_rot, t_tmp[1]], writes=[t_rot])

            range_reduce(sin2, 0.0)
            sc.op("act", lambda e: e.activation(out=sin2[:], in_=sin2[:], func=AF.Sin, scale=cst[:, C_NSGN:C_NSGN + 1]), reads=[t_rot, t_cst], writes=[t_rot])
            range_reduce(cos2, 0.5 * np.pi)
            sc.op("act", lambda e: e.activation(out=cos2[:], in_=cos2[:], func=AF.Sin), reads=[t_rot], writes=[t_rot])

            fus = [take_unit() for _ in range(7)]

            def fchunk(u, half, bb):
                s = uslot(fus[u])
                for k in range(8):
                    sc.op("pe", lambda e, k=k: e.matmul(bank(bb), lhsT=ring[:, s, half * 1024 + k * 128: half * 1024 + (k + 1) * 128], rhs=aT[:, abuf_in, k, :], start=(k == 0), stop=(k == 7)),
                          reads=[t_ring[s]] + rd_a, writes=[t_bank[bb]])

            for u in range(4):
                hp = u % 2
                isk = u >= 2
                b0, b1 = (0, 1) if u % 2 == 0 else (2, 3)
                fchunk(u, 0, b0)
                fchunk(u, 1, b1)
                cb = C_BIASF + 2 * u
                sc.op("dve", lambda e, b0=b0, cb=cb: e.scalar_tensor_tensor(out=rA[:], in0=bank(b0), scalar=cst[:, cb:cb + 1], in1=cos2[:], op0=ALU.add, op1=ALU.mult),
                      reads=[t_bank[b0], t_cst, t_rot], writes=[t_tmp[0]])
                sc.op("dve", lambda e, b1=b1, cb=cb: e.scalar_tensor_tensor(out=rB[:], in0=bank(b1), scalar=cst[:, cb + 1:cb + 2], in1=sin2[:], op0=ALU.add, op1=ALU.mult),
                      reads=[t_bank[b1], t_cst, t_rot], writes=[t_tmp[1]])
                sc.op("pool", lambda e: e.tensor_tensor(out=rA[:], in0=rA[:], in1=rB[:], op=ALU.add), reads=[t_tmp[1]], writes=[t_tmp[0]])
                dcol = (C_KDEC if isk else C_QDEC) + hp * 128
                dst = (kiT if isk else qdT)[:, hp, :].rearrange("p (n c) -> p n c", n=NT)
                dec = cst[:, dcol:dcol + 128].unsqueeze(1).broadcast_to([128, NT, 128])
                sc.op("pool", lambda e, dst=dst, dec=dec: e.tensor_tensor(out=dst, in0=rA[:].rearrange("p (n c) -> p n c", n=NT), in1=dec, op=ALU.mult),
                      reads=[t_tmp[0], t_cst], writes=[(t_ki if isk else t_qd)[hp]])
            for c in range(4):
                bb = c % 4
                fchunk(4 + c // 2, c % 2, bb)
                cb = C_BIASF + 8 + c
                sc.op("act", lambda e, bb=bb, cb=cb, c=c: e.activation(out=sqT[:, c, :], in_=bank(bb), func=AF.Identity, bias=cst[:, cb:cb + 1], scale=1.0),
                      reads=[t_bank[bb], t_cst], writes=[t_sq[c]])
            par = blk % 2
            for kv in range(2):
                bb = kv
                fchunk(6, kv, bb)
                cb = C_BIASF + 12 + kv
                sc.op("act", lambda e, bb=bb, cb=cb, kv=kv: e.activation(out=skT[:, kv, par * TB:(par + 1) * TB], in_=bank(bb), func=AF.Identity, bias=cst[:, cb:cb + 1], scale=1.0),
                      reads=[t_bank[bb], t_cst], writes=[t_sk[kv][par * 4 + n] for n in range(4)])
            release_units(fus)

            if MIX_LEVEL < 2:
                skip_units(9)
                return
            tus = [take_unit() for _ in range(5)]
            def make_swa(n):
                gn = blk * NT + n
                g8 = gn % 8
                p8 = (gn - 1) % 8
                a2 = n % 2
                cs = slice(n * 128, (n + 1) * 128)
                mcol = C_SWAM0 if gn == 0 else C_SWAM
                SBANK = ((2, 3), (0, 1))
                sw = swst[:, gn % 2]
                tsw = t_swst[gn % 2]

                def swa_front():
                    for rnd in range(2):
                        kv = rnd
                        for hq in range(4):
                            hd = rnd * 4 + hq
                            c, b64 = hd // 2, (hd % 2) * 64
                            bb = SBANK[rnd][hq % 2]
                            col = (hq // 2) * 256
                            sc.op("pe", lambda e, c=c, b64=b64, bb=bb, col=col, kv=kv: e.matmul(bank(bb)[:, col:col + 128], lhsT=sqT[b64:b64 + 64, c, cs], rhs=skT[b64:b64 + 64, kv, p8 * 128:(p8 + 1) * 128], start=True, stop=True),
                                  reads=[t_sq[c], t_sk[kv][p8]], writes=[t_bank[bb]])
                            sc.op("pe", lambda e, c=c, b64=b64, bb=bb, col=col, kv=kv: e.matmul(bank(bb)[:, col + 128:col + 256], lhsT=sqT[b64:b64 + 64, c, cs], rhs=skT[b64:b64 + 64, kv, g8 * 128:(g8 + 1) * 128], start=True, stop=True),
                                  reads=[t_sq[c], t_sk[kv][g8]], writes=[t_bank[bb]])
                    mb = cst[:, mcol:mcol + 256].unsqueeze(1).broadcast_to([128, 2, 256])
                    for rnd in range(2):
                        for hh in range(2):
                            bb = SBANK[rnd][hh]
                            s0 = rnd * 4 + 2 * hh
                            sc.op("dve", lambda e, bb=bb, s0=s0: e.scalar_tensor_tensor(out=swS[:, s0:s0 + 2, :], in0=bank(bb).rearrange("p (h c) -> p h c", h=2), scalar=0.125, in1=mb, op0=ALU.mult, op1=ALU.add),
                                  reads=[t_bank[bb], t_cst], writes=[t_swS])
                    sk_ = bcs[:, B_SINK:B_SINK + 8]
                    sc.op("dve", lambda e: e.tensor_reduce(out=sw[:, :, 0], in_=swS[:], axis=AX.X, op=ALU.max), reads=[t_swS], writes=[tsw])
                    sc.op("dve", lambda e: e.tensor_tensor(out=sw[:, :, 0], in0=sw[:, :, 0], in1=sk_, op=ALU.max), reads=[tsw, t_cst], writes=[tsw])
                    sc.op("dve", lambda e: e.tensor_scalar(out=sw[:, :, 1], in0=sw[:, :, 0], scalar1=-1.0, scalar2=None, op0=ALU.mult), reads=[tsw], writes=[tsw])
                    sc.op("dve", lambda e: e.tensor_tensor(out=sw[:, :, 2], in0=sk_, in1=sw[:, :, 0], op=ALU.subtract), reads=[tsw, t_cst], writes=[tsw])
                def swa_front_b():
                    for s_ in range(8):
                        sc.op("act", lambda e, s_=s_: e.activation(out=swP[:, s_, :], in_=swS[:, s_, :], func=AF.Exp, bias=sw[:, s_, 1:2], scale=1.0, accum_out=sw[:, s_, 3:4]),
                              reads=[t_swS, tsw], writes=[t_swP, tsw])
                    sc.op("act", lambda e: e.activation(out=sw[:, :, 2], in_=sw[:, :, 2], func=AF.Exp), reads=[tsw], writes=[tsw])

                def swa_back():
                    for s_ in range(8):
                        pb = 4 + s_ // 4
                        for kb in range(2):
                            sc.op("pe", lambda e, s_=s_, kb=kb, pb=pb: e.transpose(bankb(pb)[:, ((s_ % 4) * 2 + kb) * 128:((s_ % 4) * 2 + kb + 1) * 128], swP[:, s_, kb * 128:(kb + 1) * 128], ident[:]),
                                  reads=[t_swP, t_cst], writes=[t_bank[pb]])
                    for rnd in range(2):
                        sc.op("act", lambda e, rnd=rnd: e.copy(out=swPT[:, rnd * 8:(rnd + 1) * 8, :].rearrange("p a b -> p (a b)"), in_=bankb(4 + rnd)), reads=[t_bank[4 + rnd]], writes=[t_swPT])

                def swa_back_b():
                    sc.op("dve", lambda e: e.tensor_tensor(out=sw[:, :, 3], in0=sw[:, :, 3], in1=sw[:, :, 2], op=ALU.add), reads=[tsw], writes=[tsw])
                    sc.op("dve", lambda e: e.reciprocal(out=sw[:, :, 4], in_=sw[:, :, 3]), reads=[tsw], writes=[tsw])
                    for s_ in range(8):
                        kv = s_ // 4
                        sc.op("pe", lambda e, s_=s_, kv=kv: e.matmul(bank(7)[:, s_ * 64:(s_ + 1) * 64], lhsT=swPT[:, s_ * 2, :], rhs=svt[:, p8, kv * 64:(kv + 1) * 64], start=True, stop=False),
                              reads=[t_swPT, t_sv[p8]], writes=[t_bank[7]])
                        sc.op("pe", lambda e, s_=s_, kv=kv: e.matmul(bank(7)[:, s_ * 64:(s_ + 1) * 64], lhsT=swPT[:, s_ * 2 + 1, :], rhs=svt[:, g8, kv * 64:(kv + 1) * 64], start=False, stop=True),
                              reads=[t_swPT, t_sv[g8]], writes=[t_bank[7]])
                    sc.op("dve", lambda e: e.tensor_tensor(out=mixt[:, a2, 512:1024].rearrange("p (h d) -> p h d", h=8), in0=bank(7).rearrange("p (h d) -> p h d", h=8),
                                                           in1=sw[:, :, 4].unsqueeze(2).broadcast_to([128, 8, 64]), op=ALU.mult),
                          reads=[t_bank[7], tsw], writes=[t_mix[a2]])

                return swa_front, swa_front_b, swa_back, swa_back_b

            swa_fb = [make_swa(n) for n in range(NT)]

            def tokmajor(n):
                gn = blk * NT + n
                g8 = gn % 8
                a2 = n % 2
                tok_major_mm(n, tus[0:2], 4, 512, abuf_in, 4)
                sc.op("dve", lambda e, a2=a2: e.tensor_tensor(out=vtok[:, a2, :], in0=bank(4), in1=bcs[:, B_BIAS:B_BIAS + 512], op=ALU.add), reads=[t_bank[4], t_cst], writes=[t_v[a2]])
                tok_major_mm(n, tus[4:5], 8, 128, abuf_in, 6)
                sc.op("dve", lambda e, g8=g8: e.tensor_tensor(out=svt[:, g8, :], in0=bank(6)[:, 0:128], in1=bcs[:, B_BIAS + 1024:B_BIAS + 1152], op=ALU.add), reads=[t_bank[6], t_cst], writes=[t_sv[g8]])
                tok_major_mm(n, tus[2:4], 4, 512, abuf_in, 5)
                sc.op("dve", lambda e, a2=a2: e.tensor_tensor(out=rgs[:, a2, :], in0=bank(5), in1=bcs[:, B_BIAS + 512:B_BIAS + 1024], op=ALU.add), reads=[t_bank[5], t_cst], writes=[t_rg[a2]])
                sc.op("act", lambda e, a2=a2: e.activation(out=sg[:, 0, :], in_=rgs[:, a2, :], func=AF.Tanh, scale=0.5), reads=[t_rg[a2]], writes=[t_sg[0]])
                sc.op("dve", lambda e, a2=a2: e.scalar_tensor_tensor(out=rgs[:, a2, :], in0=sg[:, 0, :], scalar=1.0, in1=rgs[:, a2, :], op0=ALU.add, op1=ALU.mult), reads=[t_sg[0], t_rg[a2]], writes=[t_rg[a2]])
                if n == NT - 1:
                    release_units(tus)

            def tile_body(n):
                gn = blk * NT + n
                g8 = gn % 8
                p8 = (gn - 1) % 8
                a2 = n % 2
                cs = slice(n * 128, (n + 1) * 128)
                mcol = C_SWAM0 if gn == 0 else C_SWAM
                SBANK = ((2, 3), (0, 1))

                if MIX_LEVEL >= 4 and n == 0:
                    swa_fb[0][0]()
                    swa_fb[0][1]()
                if n == 0 or MIX_LEVEL < 4:
                    tokmajor(n)
                cs = slice(n * 128, (n + 1) * 128)
                if MIX_LEVEL < 3:
                    return
                for hp in range(2):
                    sc.op("pe", lambda e, hp=hp: e.transpose(bankb(3)[:, 512 + hp * 128:512 + (hp + 1) * 128], kiT[:, hp, cs], ident[:]), reads=[t_ki[hp], t_cst], writes=[t_bank[3]])
                sc.op("act", lambda e, a2=a2: e.copy(out=kitok[:, a2, :], in_=bankb(3)[:, 512:768]), reads=[t_bank[3]], writes=[t_kit[a2]])
                if MIX_SUB < 2:
                    return
                for hd in range(4):
                    hp, b64 = hd // 2, (hd % 2) * 64
                    sbk = 7 if hd % 2 == 0 else 3
                    sc.op("pe", lambda e, hd=hd, hp=hp, b64=b64, sbk=sbk: e.matmul(bank(sbk)[:, (hd // 2) * 128:(hd // 2 + 1) * 128], lhsT=kiT[b64:b64 + 64, hp, cs], rhs=qdT[b64:b64 + 64, hp, cs], start=True, stop=True),
                          reads=[t_ki[hp], t_qd[hp]], writes=[t_bank[sbk]])
                caus = cst[:, C_CAUS:C_CAUS + 128].unsqueeze(1).broadcast_to([128, 2, 128])
                for par, sbk in ((0, 7), (1, 3)):
                    sc.op("dve", lambda e, a2=a2, par=par, sbk=sbk: e.tensor_tensor(out=sTm[:, a2, par * 256:(par + 1) * 256].rearrange("p (h c) -> p h c", h=2), in0=bank(sbk)[:, 0:256].rearrange("p (h c) -> p h c", h=2), in1=caus, op=ALU.mult),
                          reads=[t_bank[sbk], t_cst], writes=[t_sTm[a2]])
                if MIX_SUB < 3:
                    return
                for hp in range(2):
                    sc.op("pe", lambda e, hp=hp, a2=a2: e.matmul(bank(1)[:, hp * 256:(hp + 1) * 256], lhsT=kitok[:, a2, hp * 128:(hp + 1) * 128], rhs=vtok[:, a2, hp * 256:(hp + 1) * 256], start=True, stop=True),
                          reads=[t_kit[a2], t_v[a2]], writes=[t_bank[1]])
                if MIX_SUB < 4:
                    return
                sa = gn % 2
                for hd in range(4):
                    hp, b64 = hd // 2, (hd % 2) * 64
                    sc.op("pe", lambda e, hd=hd, a2=a2: e.matmul(bank(0)[:, hd * 128:(hd + 1) * 128], lhsT=sTm[:, a2, ((hd % 2) * 2 + hd // 2) * 128:((hd % 2) * 2 + hd // 2 + 1) * 128], rhs=vtok[:, a2, hd * 128:(hd + 1) * 128], start=True, stop=False),
                          reads=[t_sTm[a2], t_v[a2]], writes=[t_bank[0]])
                    sc.op("pe", lambda e, hd=hd, hp=hp, b64=b64, sa=sa: e.matmul(bank(0)[:, hd * 128:(hd + 1) * 128], lhsT=qdT[b64:b64 + 64, hp, cs], rhs=stB[b64:b64 + 64, sa, hp, :], start=False, stop=True),
                          reads=[t_qd[hp], t_sB[sa][hp]], writes=[t_bank[0]])
                if MIX_LEVEL >= 5 and n >= 1:
                    mix_T(n - 1)
                if MIX_SUB < 5:
                    return
                for hp in range(2):
                    gc = cst[:, C_GAMC + hp:C_GAMC + hp + 1]
                    for hh in range(2):
                        ps_ = slice(hh * 64, hh * 64 + 64)
                        sc.op("dve", lambda e, hp=hp, hh=hh, ps_=ps_: e.scalar_tensor_tensor(out=stU[ps_, hp, :], in0=stU[ps_, hp, :], scalar=cst[ps_, C_GAMC + hp:C_GAMC + hp + 1],
                                                                                         in1=bank(1)[ps_, hp * 256 + hh * 128:hp * 256 + hh * 128 + 128], op0=ALU.mult, op1=ALU.add),
                              reads=[t_bank[1], t_cst], writes=[t_U[hp]])
                    sc.op("dve", lambda e, hp=hp, gc=gc, sa=sa: e.tensor_scalar(out=stB[:, 1 - sa, hp, :], in0=stU[:, hp, :], scalar1=gc, scalar2=None, op0=ALU.mult),
                          reads=[t_U[hp], t_cst], writes=[t_sB[1 - sa][hp]])
                if MIX_SUB < 6:
                    return
                sc.op("act", lambda e: e.copy(out=osb[:], in_=bank(0)), reads=[t_bank[0]], writes=[t_osb])
                sc.op("act", lambda e: e.activation(out=osq[:], in_=osb[:], func=AF.Square), reads=[t_osb], writes=[t_osq])
                o3 = osb[:].rearrange("p (h c) -> p h c", h=4)
                sc.op("dve", lambda e: e.tensor_reduce(out=gst[:, :, 0], in_=o3, axis=AX.X, op=ALU.add), reads=[t_osb], writes=[t_gst])
                sc.op("dve", lambda e: e.tensor_reduce(out=gst[:, :, 1], in_=osq[:].rearrange("p (h c) -> p h c", h=4), axis=AX.X, op=ALU.add), reads=[t_osq], writes=[t_gst])
                sc.op("pool", lambda e: e.tensor_scalar(out=gst[:, :, 0], in0=gst[:, :, 0], scalar1=1.0 / 128.0, scalar2=None, op0=ALU.mult), reads=[t_gst], writes=[t_gst])
                sc.op("pool", lambda e: e.tensor_tensor(out=gst[:, :, 2], in0=gst[:, :, 0], in1=gst[:, :, 0], op=ALU.mult), reads=[t_gst], writes=[t_gst])
                sc.op("pool", lambda e: e.tensor_scalar(out=gst[:, :, 1], in0=gst[:, :, 1], scalar1=1.0 / 128.0, scalar2=EPS, op0=ALU.mult, op1=ALU.add), reads=[t_gst], writes=[t_gst])
                sc.op("pool", lambda e: e.tensor_tensor(out=gst[:, :, 1], in0=gst[:, :, 1], in1=gst[:, :, 2], op=ALU.subtract), reads=[t_gst], writes=[t_gst])
                sc.op("pool", lambda e: e.tensor_tensor(out=gst[:, :, 3], in0=gst[:, :, 1], in1=cst[:, C_NHALF:C_NHALF + 1].broadcast_to([128, 4]), op=ALU.pow), reads=[t_gst, t_cst], writes=[t_gst])
                sc.op("pool", lambda e: e.tensor_scalar(out=gst[:, :, 3], in0=gst[:, :, 3], scalar1=0.5, scalar2=None, op0=ALU.mult), reads=[t_gst], writes=[t_gst])
                sc.op("dve", lambda e: e.tensor_tensor(out=o3, in0=o3, in1=gst[:, :, 0].unsqueeze(2).broadcast_to([128, 4, 128]), op=ALU.subtract), reads=[t_gst, t_osb], writes=[t_osb])
                sc.op("dve", lambda e: e.tensor_tensor(out=o3, in0=o3, in1=gst[:, :, 3].unsqueeze(2).broadcast_to([128, 4, 128]), op=ALU.mult), reads=[t_gst, t_osb], writes=[t_osb])
                sc.op("pool", lambda e, a2=a2: e.tensor_tensor(out=mixt[:, a2, 0:512], in0=osb[:], in1=rgs[:, a2, :], op=ALU.mult), reads=[t_osb, t_rg[a2]], writes=[t_mix[a2]])

                if MIX_LEVEL >= 4:
                    if n + 1 < NT:
                        swa_fb[n + 1][0]()
                    swa_fb[n][2]()
                    if n + 1 < NT:
                        tokmajor(n + 1)
                    swa_fb[n][3]()
                    if n + 1 < NT:
                        swa_fb[n + 1][1]()
            def mix_T(n):
                a2 = n % 2
                for k in range(8):
                    sc.op("pe", lambda e, k=k: e.transpose(bankb(6)[:, k * 128:(k + 1) * 128], mixt[:, a2, k * 128:(k + 1) * 128], ident[:]), reads=[t_mix[a2], t_cst], writes=[t_bank[6]])
                sc.op("act", lambda e: e.copy(out=aT[:, abuf_out, :, n * 128:(n + 1) * 128], in_=bankb(6).rearrange("p (k c) -> p k c", k=8)), reads=[t_bank[6]], writes=[t_aT[abuf_out][n]])

            for n in range(NT):
                tile_body(n)
            if MIX_LEVEL >= 5:
                mix_T(NT - 1)
            if MIX_LEVEL < 5:
                skip_units(4)
                return
            ous = [take_unit() for _ in range(4)]
            for i in range(NT):
                bk = 4 + 2 * (i % 2)
                tok_major_mm(i, ous, 2, 1024, abuf_out, bk)
                if i == NT - 1:
                    release_units(ous)
                if next_pre is not None and i >= 2:
                    next_pre(i - 2)
                postnorm_add(i, bk, 1)
            if next_pre is not None:
                next_pre(NT - 2)
                next_pre(NT - 1)

        def ple_prep(blk):
            t0 = blk * TB
            for i in range(NT):
                a2 = i % 2
                sc.dma("sp", ptok[:, a2, :], pin[t0 + i * 128:t0 + (i + 1) * 128, :], writes=[t_ptok[a2]], pool=("pl", 2))
                sc.op("pool", lambda e, a2=a2: e.tensor_copy(out=pbf[:, a2, :], in_=ptok[:, a2, :]), reads=[t_ptok[a2]], writes=[t_pbf[a2]])
                for k in range(2):
                    sc.op("pe", lambda e, k=k, a2=a2: e.transpose(bankb(3)[:, k * 128:(k + 1) * 128], pbf[:, a2, k * 128:(k + 1) * 128], ident[:]), reads=[t_pbf[a2], t_cst], writes=[t_bank[3]])
                sc.op("act", lambda e, i=i: e.copy(out=pT[:, :, i * 128:(i + 1) * 128], in_=bankb(3)[:, 0:256].rearrange("p (k c) -> p k c", k=2)), reads=[t_bank[3]], writes=[t_pT[i]])

        def ple(blk, abuf, do_pre=True, after_tile=None):
            gus = [take_unit() for _ in range(4)]
            pu = [take_unit()]
            if do_pre:
                for i in range(NT):
                    prenorm_T(i, 0, abuf, do_norm=False)
            for i in range(NT):
                ti = i % 2
                tok_major_mm(i, gus, 2, 1024, abuf, 4)
                for half in range(2):
                    for k in range(2):
                        s = uslot(pu[0])
                        sc.op("pe", lambda e, k=k, half=half, s=s, i=i: e.matmul(bank(6 + half), lhsT=pT[:, k, i * 128:(i + 1) * 128], rhs=ring[:, s, k * 1024 + half * 512:k * 1024 + half * 512 + 512], start=(k == 0), stop=(k == 1)),
                              reads=[t_ring[s], t_pT[i]], writes=[t_bank[6 + half]])
                if i == NT - 1:
                    release_units(gus + pu)
                sc.op("act", lambda e, ti=ti: e.activation(out=tmp[:, ti, :], in_=psum[2][:, :], func=AF.Tanh, scale=0.5), reads=[t_bank[4], t_bank[5]], writes=[t_tmp[ti]])
                sc.op("dve", lambda e, ti=ti: e.scalar_tensor_tensor(out=tmp[:, ti, :], in0=tmp[:, ti, :], scalar=1.0, in1=psum[3][:, :], op0=ALU.add, op1=ALU.mult),
                      reads=[t_tmp[ti], t_bank[6], t_bank[7]], writes=[t_tmp[ti]])
                ms, tms = new_ms()
                r, tr = new_r()
                sc.op("act", lambda e, ti=ti, ms=ms: e.activation(out=xn[:, ti, :], in_=tmp[:, ti, :], func=AF.Square, scale=1.0 / 32.0, accum_out=ms[:, 0:1]), reads=[t_tmp[ti]], writes=[t_xn[ti], tms])
                sc.op("pool", lambda e, ms=ms, r=r: e.tensor_scalar(out=r[:, 0:1], in0=ms[:, 0:1], scalar1=4.0 * EPS, scalar2=None, op0=ALU.add), reads=[tms], writes=[tr])
                sc.op("pool", lambda e, r=r: e.tensor_tensor(out=r[:, 1:2], in0=r[:, 0:1], in1=cst[:, C_NHALF:C_NHALF + 1], op=ALU.pow), reads=[tr, t_cst], writes=[tr])
                gb = bcs[:, B_GPOST + 3 * 1024:B_GPOST + 4 * 1024]
                sc.op("dve", lambda e, ti=ti, r=r: e.scalar_tensor_tensor(out=tmp[:, ti, :], in0=tmp[:, ti, :], scalar=r[:, 1:2], in1=gb, op0=ALU.mult, op1=ALU.mult), reads=[t_tmp[ti], tr, t_cst], writes=[t_tmp[ti]])
                sc.op("pool", lambda e, i=i, ti=ti: e.tensor_tensor(out=h[:, i, :], in0=h[:, i, :], in1=tmp[:, ti, :], op=ALU.add), reads=[t_tmp[ti]], writes=[t_h[i]])
                if after_tile is not None:
                    after_tile(i)

        for blk in range(NBLK_RUN):
            t0 = blk * TB
            full = STOP_AFTER >= 4
            if blk == 0 or not full:
                for i in range(NT):
                    sc.dma("sp", h[:, i, :], x[t0 + i * 128:t0 + (i + 1) * 128, :], writes=[t_h[i]], pool=("xl", 4))
            if blk == 0:
                for g in range(RING):
                    load_unit(g)
            PIPE = True
            ffn(0, 0, 0, next_pre=(lambda i: prenorm_T(i, 1, 1)) if (PIPE and STOP_AFTER >= 2) else None)
            if STOP_AFTER >= 2:
                mixer(blk, 1, 0, do_pre=not PIPE, next_pre=(lambda i: prenorm_T(i, 2, 1)) if (PIPE and STOP_AFTER >= 3 and MIX_LEVEL >= 5) else None)
            else:
                skip_units(16)
            if STOP_AFTER >= 4:
                ple_prep(blk)
            if STOP_AFTER >= 3:
                ffn(2, 2, 1, do_pre=not (PIPE and MIX_LEVEL >= 5), next_pre=(lambda i: prenorm_T(i, 0, 0, do_norm=False)) if (PIPE and STOP_AFTER >= 4) else None)
            else:
                skip_units(33)
            def store_load(i, blk=blk, t0=t0):
                sc.dma("sp", y[t0 + i * 128:t0 + (i + 1) * 128, :], h[:, i, :], reads=[t_h[i]], pool=("st", 4))
                if blk + 1 < NBLK_RUN:
                    t1 = t0 + TB
                    sc.dma("sp", h[:, i, :], x[t1 + i * 128:t1 + (i + 1) * 128, :], writes=[t_h[i]], pool=("xl", 4))

            if full:
                ple(blk, 0, do_pre=not PIPE, after_tile=store_load)
            else:
                skip_units(5)
                for i in range(NT):
                    sc.dma("sp", y[t0 + i * 128:t0 + (i + 1) * 128, :], h[:, i, :], reads=[t_h[i]], pool=("st", 4))
        sc.emit(st)
    return nc


def _fpair(W, cols_a, cols_b):
    out = np.empty((128, 2, 8, 128), np.float32)
    for s_, cols in enumerate((cols_a, cols_b)):
        out[:, s_] = W[:, cols].reshape(8, 128, 128).transpose(1, 0, 2)
    return out.reshape(128, UE)


def _tunit(W, k0, kk, n0, n):
    blk = W[k0 * 128:(k0 + kk) * 128, n0:n0 + n].reshape(kk, 128, n).transpose(1, 0, 2).reshape(128, kk * n)
    out = np.zeros((128, UE), np.float32)
    out[:, :kk * n] = blk
    return out


def _ffn_units(wg, wu_, wd):
    us = []
    for j in range(22):
        c = np.arange(j * 128, (j + 1) * 128)
        us.append(_fpair(np.concatenate([wg, wu_], axis=1), c, c + DFF))
    for m in range(11):
        us.append(_tunit(wd, 2 * m, 2, 0, 1024))
    return us


def _swap_cols(base):
    cols = []
    for hh in range(4):
        cols += list(range(base + hh * 64 + 32, base + hh * 64 + 64)) + list(range(base + hh * 64, base + hh * 64 + 32))
    return np.array(cols)


def _win_chunks():
    rq = np.arange(0, 256)
    rqs = _swap_cols(0)
    rk = np.arange(256, 512)
    rks = _swap_cols(256)
    sq = np.arange(1536, 2048)
    sk0 = np.concatenate([np.arange(2048, 2112)] * 2)
    sk1 = np.concatenate([np.arange(2112, 2176)] * 2)
    pairs = [(rq[0:128], rqs[0:128]), (rq[128:256], rqs[128:256]), (rk[0:128], rks[0:128]), (rk[128:256], rks[128:256]),
             (sq[0:128], sq[128:256]), (sq[256:384], sq[384:512]), (sk0, sk1)]
    return pairs


def _host_prep(inp):
    f = lambda k: np.asarray(inp[k], np.float32)[0]
    units = []
    units += _ffn_units(f("ffn1_w_gate"), f("ffn1_w_up"), f("ffn1_w_down"))
    w_in = f("w_in")
    pairs = _win_chunks()
    for a, b in pairs:
        units.append(_fpair(w_in, a, b))
    units.append(_tunit(w_in, 0, 4, 512, 512))
    units.append(_tunit(w_in, 4, 4, 512, 512))
    units.append(_tunit(w_in, 0, 4, 1024, 512))
    units.append(_tunit(w_in, 4, 4, 1024, 512))
    units.append(_tunit(w_in, 0, 8, 2176, 128))
    w_out = f("w_out")
    rowperm = list(range(512))
    for s_ in range(8):
        hd = (s_ // 4) * 4 + PERM[s_ % 4]
        rowperm += list(range(512 + hd * 64, 512 + (hd + 1) * 64))
    w_out = w_out[np.array(rowperm)]
    for m in range(4):
        units.append(_tunit(w_out, 2 * m, 2, 0, 1024))
    units += _ffn_units(f("ffn2_w_gate"), f("ffn2_w_up"), f("ffn2_w_down"))
    pg = f("ple_w_gate")
    for m in range(4):
        units.append(_tunit(pg, 2 * m, 2, 0, 1024))
    units.append(_tunit(f("ple_w_proj"), 0, 2, 0, 1024))
    assert len(units) == NU
    wu = np.ascontiguousarray(np.stack(units).reshape(NU * 128, UE))

    cst = np.zeros((128, NC_), np.float32)
    for s_, name in enumerate(("ffn1_pre_g", "mix_pre_g", "ffn2_pre_g")):
        cst[:, C_GPRE + s_ * 8:C_GPRE + s_ * 8 + 8] = f(name).reshape(8, 128).T
    b_in = f("b_in")
    bcols = []
    for a, b in pairs[0:4]:
        bcols += [b_in[a], b_in[b]]
    for a, b in pairs[4:7]:
        bcols += [b_in[a], b_in[b]]
    cst[:, C_BIASF:C_BIASF + 14] = np.stack(bcols, axis=1)
    pidx = np.arange(128)
    cst[:, C_INVF] = (10000.0 ** (-(pidx % 32).astype(np.float32) / 32.0)).astype(np.float32)
    sgn = np.where((pidx % 64) < 32, -1.0, 1.0).astype(np.float32)
    cst[:, C_NSGN] = sgn
    cst[:, C_SGNPI] = sgn * np.float32(np.pi)
    hidx = np.arange(4, dtype=np.float64)
    lg = np.log(1.0 - 2.0 ** (-5.0 - hidx))
    for hp in range(2):
        hsel = hp * 2 + pidx // 64
        cst[:, C_GAMC + hp] = np.exp(lg[hsel] * 128.0)
        cc = np.arange(128, dtype=np.float64)
        cst[:, C_QDEC + hp * 128:C_QDEC + (hp + 1) * 128] = np.exp(lg[hsel][:, None] * (cc[None, :] + 1.0))
        cst[:, C_KDEC + hp * 128:C_KDEC + (hp + 1) * 128] = 0.125 * np.exp(-lg[hsel][:, None] * (cc[None, :] + 1.0))
    m = np.arange(128)
    cst[:, C_CAUS:C_CAUS + 128] = (m[None, :] >= m[:, None]).astype(np.float32)
    q = np.arange(128)[:, None]
    kk = np.arange(256)[None, :]
    rel = (q + 128) - kk
    valid = (rel >= 0) & (rel < 128)
    cst[:, C_SWAM:C_SWAM + 256] = np.where(valid, 0.0, NEG)
    cst[:, C_SWAM0:C_SWAM0 + 256] = np.where(valid & (kk >= 128), 0.0, NEG)
    cst[:, C_NHALF] = -0.5
    cst[:, C_PI] = np.float32(np.pi)

    bc = np.zeros((128, NB_), np.float32)
    for gi, name in enumerate(("ffn1_post_g", "mix_post_g", "ffn2_post_g", "ple_norm_g")):
        bc[:, B_GPOST + gi * 1024:B_GPOST + (gi + 1) * 1024] = f(name)[None, :]
    bc[:, B_BIAS:B_BIAS + 512] = b_in[512:1024][None, :]
    bc[:, B_BIAS + 512:B_BIAS + 1024] = b_in[1024:1536][None, :]
    bc[:, B_BIAS + 1024:B_BIAS + 1152] = b_in[2176:2304][None, :]
    sk_perm = [r * 4 + PERM[j] for r in range(2) for j in range(4)]
    bc[:, B_SINK:B_SINK + 8] = f("swa_sinks")[sk_perm][None, :]
    ident = np.eye(128, dtype=np.float32).astype(ml_dtypes.bfloat16)
    return wu, cst, bc, ident


_CACHE = {}


def kernel(**inputs):
    wu, cst, bc, ident = _host_prep(inputs)
    x = np.asarray(inputs["x"], np.float32)
    p = np.asarray(inputs["p"], np.float32)[0]
    pos = np.asarray(inputs["positions"], np.int32)
    if "nc" not in _CACHE:
        _CACHE["nc"] = build_program()
    nc = _CACHE["nc"]
    in_maps = []
    for b in range(8):
        in_maps.append({"x": np.ascontiguousarray(x[b]), "p": np.ascontiguousarray(p[b]), "pos": np.ascontiguousarray(pos[b:b + 1]),
                        "wu": wu, "cst": cst, "bc": bc, "ident": ident})
    res = run_bass_kernel_spmd(nc, in_maps, core_ids=list(range(8)))
    out = np.stack([np.asarray(r["y"], np.float32) for r in res.results], axis=0)
    return out
```
